# Optimizing a Trainium2 kernel written in Bass

```python
import jax, jax.numpy as jnp
from jax import lax
import numpy as np

D_MODEL = 1024
BATCH = 16
SEQ = 2048
DEPTH = 2
DEC_BATCH = 8
DEC_SEQ = 64
PAST_LEN = 4096

CHUNK = 64
W_A = D_MODEL
W_B = D_MODEL
W_C = D_MODEL
CONV_A = 3
CONV_B = 31
MLP_CHUNK = 128
C_GROUPS = 8
C_GROUP_DIM = W_C // C_GROUPS
N_BRANCH = 3
EPS = 1e-6
SPLITS = [W_A, W_A, W_A, W_A, 2 * W_B, W_B, W_C, W_C, W_C, D_MODEL, D_MODEL, D_MODEL]
PROJ_DIM = sum(SPLITS)
SPLIT_IDX = [int(i) for i in np.cumsum(SPLITS)[:-1]]

kernel_name = "parallel_conv_gmlp_stream_step"


def rmsnorm(x, g):
    xf = x.astype(jnp.float32)
    y = xf * lax.rsqrt(jnp.mean(xf * xf, axis=-1, keepdims=True) + EPS)
    return (y * g.astype(jnp.float32)).astype(x.dtype)


def layernorm(x, g, b):
    xf = x.astype(jnp.float32)
    mu = jnp.mean(xf, axis=-1, keepdims=True)
    xc = xf - mu
    y = xc * lax.rsqrt(jnp.mean(xc * xc, axis=-1, keepdims=True) + EPS)
    return (y * g.astype(jnp.float32) + b.astype(jnp.float32)).astype(x.dtype)


def causal_dwconv(prev, x, w):
    xp = jnp.concatenate([prev.astype(x.dtype), x], axis=1)
    y = lax.conv_general_dilated(
        xp, w[:, None, :].astype(x.dtype), window_strides=(1,), padding="VALID",
        dimension_numbers=("NWC", "WIO", "NWC"), feature_group_count=x.shape[-1])
    return y, xp[:, -(w.shape[0] - 1):]


def chunk_spatial_mix(v, w_s, b_s):
    bn, t, _ = v.shape
    pad = (-t) % MLP_CHUNK
    vp = jnp.pad(v, ((0, 0), (0, pad), (0, 0)))
    n = (t + pad) // MLP_CHUNK
    vp = vp.reshape(bn, n, MLP_CHUNK, C_GROUPS, C_GROUP_DIM)
    mask = jnp.tril(jnp.ones((MLP_CHUNK, MLP_CHUNK), dtype=bool))
    wm = jnp.where(mask, w_s, jnp.zeros_like(w_s)).astype(v.dtype)
    s = jnp.einsum("gpq,bnqgc->bnpgc", wm, vp) + b_s.T[:, :, None].astype(v.dtype)
    return s.reshape(bn, n * MLP_CHUNK, W_C)[:, :t]


def mixer_layer(x, prev_a, prev_b, norm_g, w_in, conv_a_w, conv_b_w, conv_b_b,
                ln_b_g, ln_b_b, ln_c_g, ln_c_b, w_s, b_s, gate_b, w_branch, w_out):
    h = rmsnorm(x, norm_g)
    proj = jnp.einsum("btd,de->bte", h, w_in.astype(h.dtype))
    (a_b, a_c, a_x, a_z, b_in, b_z, c_u, c_v, c_z, g_a, g_b, g_c) = jnp.split(proj, SPLIT_IDX, axis=-1)
    conv_a_out, new_a = causal_dwconv(prev_a, a_c * a_x, conv_a_w)
    y_a = a_b * conv_a_out * jax.nn.silu(a_z)
    glu = b_in[..., :W_B] * jax.nn.sigmoid(b_in[..., W_B:])
    conv_b_out, new_b = causal_dwconv(prev_b, glu, conv_b_w)
    y_b = jax.nn.silu(layernorm(conv_b_out + conv_b_b.astype(x.dtype), ln_b_g, ln_b_b)) * jax.nn.silu(b_z)
    u = jax.nn.gelu(c_u)
    v = layernorm(jax.nn.gelu(c_v), ln_c_g, ln_c_b)
    y_c = u * chunk_spatial_mix(v, w_s, b_s) * jax.nn.silu(c_z)
    gb = gate_b.astype(x.dtype)
    m = (jax.nn.sigmoid(g_a + gb[0]) * (y_a @ w_branch[0].astype(x.dtype))
         + jax.nn.sigmoid(g_b + gb[1]) * (y_b @ w_branch[1].astype(x.dtype))
         + jax.nn.sigmoid(g_c + gb[2]) * (y_c @ w_branch[2].astype(x.dtype)))
    x = x + m @ w_out.astype(x.dtype)
    return x, new_a, new_b, v


def setup_inputs(seed: int = 0) -> dict:
    key = jax.random.key(seed)
    ks = jax.random.split(key, 20)
    nrm = lambda k, s: jax.random.normal(k, s, jnp.float32)
    return {
        "x_prompt": nrm(ks[0], (BATCH, SEQ, D_MODEL)),
        "x_sample": nrm(ks[1], (DEC_BATCH, DEC_SEQ, D_MODEL)),
        "cache_conv_a": nrm(ks[2], (DEPTH, DEC_BATCH, CONV_A - 1, W_A)),
        "cache_conv_b": nrm(ks[3], (DEPTH, DEC_BATCH, CONV_B - 1, W_B)),
        "norm_g": 1.0 + 0.05 * nrm(ks[4], (DEPTH, D_MODEL)),
        "w_in": nrm(ks[5], (DEPTH, D_MODEL, PROJ_DIM)) * D_MODEL ** -0.5,
        "conv_a_w": nrm(ks[6], (DEPTH, CONV_A, W_A)) * CONV_A ** -0.5,
        "conv_b_w": nrm(ks[7], (DEPTH, CONV_B, W_B)) * CONV_B ** -0.5,
        "conv_b_b": 0.02 * nrm(ks[8], (DEPTH, W_B)),
        "ln_b_g": 1.0 + 0.05 * nrm(ks[9], (DEPTH, W_B)),
        "ln_b_b": 0.02 * nrm(ks[10], (DEPTH, W_B)),
        "ln_c_g": 1.0 + 0.05 * nrm(ks[11], (DEPTH, W_C)),
        "ln_c_b": 0.02 * nrm(ks[12], (DEPTH, W_C)),
        "w_s": nrm(ks[13], (DEPTH, C_GROUPS, MLP_CHUNK, MLP_CHUNK)) * MLP_CHUNK ** -0.5,
        "b_s": 1.0 + 0.1 * nrm(ks[14], (DEPTH, C_GROUPS, MLP_CHUNK)),
        "gate_b": 0.02 * nrm(ks[15], (DEPTH, N_BRANCH, D_MODEL)),
        "w_branch": nrm(ks[16], (DEPTH, N_BRANCH, W_A, D_MODEL)) * W_A ** -0.5,
        "w_out": nrm(ks[17], (DEPTH, D_MODEL, D_MODEL)) * D_MODEL ** -0.5,
        "final_g": 1.0 + 0.05 * nrm(ks[18], (D_MODEL,)),
    }


def reference(x_prompt, x_sample, cache_conv_a, cache_conv_b, norm_g, w_in, conv_a_w, conv_b_w,
              conv_b_b, ln_b_g, ln_b_b, ln_c_g, ln_c_b, w_s, b_s, gate_b, w_branch, w_out, final_g):
    xp = x_prompt
    pa_list, pb_list = [], []
    for l in range(DEPTH):
        zero_a = jnp.zeros((xp.shape[0], CONV_A - 1, W_A), xp.dtype)
        zero_b = jnp.zeros((xp.shape[0], CONV_B - 1, W_B), xp.dtype)
        xp, na, nb, _ = mixer_layer(xp, zero_a, zero_b, norm_g[l], w_in[l], conv_a_w[l], conv_b_w[l],
                                    conv_b_b[l], ln_b_g[l], ln_b_b[l], ln_c_g[l], ln_c_b[l], w_s[l],
                                    b_s[l], gate_b[l], w_branch[l], w_out[l])
        pa_list.append(na)
        pb_list.append(nb)
    y_prompt = rmsnorm(xp, final_g)
    xs = x_sample
    sa_list, sb_list, sv_list = [], [], []
    for l in range(DEPTH):
        xs, na, nb, v = mixer_layer(xs, cache_conv_a[l], cache_conv_b[l], norm_g[l], w_in[l], conv_a_w[l],
                                    conv_b_w[l], conv_b_b[l], ln_b_g[l], ln_b_b[l], ln_c_g[l], ln_c_b[l],
                                    w_s[l], b_s[l], gate_b[l], w_branch[l], w_out[l])
        sa_list.append(na)
        sb_list.append(nb)
        sv_list.append(v)
    y_sample = rmsnorm(xs, final_g)
    new_conv_a_prompt = jnp.stack(pa_list, axis=0)
    new_conv_b_prompt = jnp.stack(pb_list, axis=0)
    new_conv_a_sample = jnp.stack(sa_list, axis=0)
    new_conv_b_sample = jnp.stack(sb_list, axis=0)
    new_chunk_v_sample = jnp.stack(sv_list, axis=0)
    return (y_prompt, y_sample, new_conv_a_prompt, new_conv_b_prompt, new_conv_a_sample, new_conv_b_sample, new_chunk_v_sample)
```

```python
import numpy as np
from contextlib import ExitStack
import concourse.bass as bass
import concourse.mybir as mybir
from concourse.bass_utils import run_bass_kernel_spmd

F32 = mybir.dt.float32
BF16 = mybir.dt.bfloat16
AF = mybir.ActivationFunctionType
ALU = mybir.AluOpType

D = 1024
NCH = 8
DEPTH = 2
PROJ = 13312
EPS = 1e-6
KA = 3
KB = 31
NSLOT = 4
NTMP = 10
NCONV_LANES = 16
R_NG, R_CBB, R_LBG, R_LBB, R_GB0 = 0, 1, 2, 3, 4
R_CA = 7
R_CB = 10
NROW = 41


class Sched:
    ENG = ("pe", "act", "dve", "pool", "sp")

    def __init__(self):
        self.streams = {e: [] for e in self.ENG}
        self.count = {e: 0 for e in self.ENG}
        self.known = {e: {} for e in self.ENG}
        self.last_w = {}
        self.readers = {}
        self.lane_n = {}
        self.bank_ptr = 0
        self.held = set()
        self.tmp_ptr = 0
        self.slot_ptr = 0
        self.slot_blk = [None] * NSLOT

    def _need(self, eng, is_dma, tok, raw):
        key, val, src = tok
        if src is not None and src == eng and not is_dma:
            if eng == "pe":
                return False
        return True

    def op(self, eng, fn, reads=(), writes=(), lane=None):
        is_dma = lane is not None
        deps = {}

        def add(tok, raw):
            if tok is None:
                return
            if not self._need(eng, is_dma, tok, raw):
                return
            key, val, _ = tok
            if deps.get(key, 0) < val:
                deps[key] = val

        for r in reads:
            add(self.last_w.get(r), True)
        for w in writes:
            add(self.last_w.get(w), False)
            for key, (val, src) in self.readers.get(w, {}).items():
                add((key, val, src), False)
        if is_dma:
            n = self.lane_n.get(lane, 0)
            if n > 0:
                add((("dma", lane), 16 * n, None), False)
            self.lane_n[lane] = n + 1
            tok = (("dma", lane), 16 * (n + 1), None)
            inc = (("dma", lane), 16)
        else:
            self.count[eng] += 1
            tok = (eng, self.count[eng], eng)
            inc = (eng, 1)
        waits = []
        kn = self.known[eng]
        for key, val in deps.items():
            if kn.get(key, 0) < val:
                kn[key] = val
                waits.append((key, val))
        for r in reads:
            d = self.readers.setdefault(r, {})
            if d.get(tok[0], (0, None))[0] < tok[1]:
                d[tok[0]] = (tok[1], tok[2])
        for w in writes:
            self.last_w[w] = tok
            self.readers[w] = {}
        self.streams[eng].append((waits, fn, inc))
        return tok

    def barrier(self, skip_prefix=None):
        allk = {}
        for e in self.ENG:
            if self.count[e] > 0:
                allk[e] = self.count[e]
        for lane, n in self.lane_n.items():
            if skip_prefix is not None and lane.startswith(skip_prefix):
                continue
            allk[("dma", lane)] = 16 * n
        for e in self.ENG:
            waits = []
            kn = self.known[e]
            for key, val in allk.items():
                if key == e and e == "pe":
                    continue
                if kn.get(key, 0) < val:
                    kn[key] = val
                    waits.append((key, val))
            if waits:
                self.streams[e].append((waits, None, None))

    def bank(self):
        while self.bank_ptr in self.held:
            self.bank_ptr = (self.bank_ptr + 1) % 8
        b = self.bank_ptr
        self.bank_ptr = (b + 1) % 8
        return b

    def bank_n(self, n):
        while True:
            b = ((self.bank_ptr + n - 1) // n) * n % 8
            if all(((b + i) not in self.held) for i in range(n)):
                break
            self.bank_ptr = (b + n) % 8
        self.bank_ptr = (b + n) % 8
        return b

    def tmp(self):
        i = self.tmp_ptr
        self.tmp_ptr = (i + 1) % NTMP
        return i


def build_program(n_seq, seq_len, with_sample=True, samp_len=64):
    assert seq_len % 512 == 0
    nc = bass.Bass("TRN2", target_bir_lowering=False)
    ntok_p = n_seq * seq_len

    def din(name, shape):
        return nc.dram_tensor(name, list(shape), F32, kind="ExternalInput").ap()

    def dout(name, shape):
        return nc.dram_tensor(name, list(shape), F32, kind="ExternalOutput").ap()

    xp = din("xp", (ntok_p, D))
    xs = din("xs", (samp_len, D))
    ca_in = din("ca", (DEPTH, KA - 1, D))
    cb_in = din("cb", (DEPTH, KB - 1, D))
    norm_g = din("norm_g", (DEPTH, D))
    w_in = din("w_in", (DEPTH, D, PROJ))
    conv_a_w = din("conv_a_w", (DEPTH, KA, D))
    conv_b_w = din("conv_b_w", (DEPTH, KB, D))
    conv_b_b = din("conv_b_b", (DEPTH, D))
    ln_b_g = din("ln_b_g", (DEPTH, D))
    ln_b_b = din("ln_b_b", (DEPTH, D))
    ln_c_g = din("ln_c_g", (DEPTH, D))
    ln_c_b = din("ln_c_b", (DEPTH, D))
    w_s = din("w_s", (DEPTH, NCH, 128, 128))
    b_s = din("b_s", (DEPTH, NCH * 128))
    gate_b = din("gate_b", (DEPTH, 3, D))
    w_branch = din("w_branch", (DEPTH, 3, D, D))
    w_out = din("w_out", (DEPTH, D, D))
    final_g = din("final_g", (D,))

    yp = dout("yp", (ntok_p, D))
    ys = dout("ys", (samp_len, D))
    nap = dout("nap", (DEPTH, n_seq, KA - 1, D))
    nbp = dout("nbp", (DEPTH, n_seq, KB - 1, D))
    nas = dout("nas", (DEPTH, KA - 1, D))
    nbs = dout("nbs", (DEPTH, KB - 1, D))
    nvs = dout("nvs", (DEPTH, samp_len, D))

    BLK = {}
    nblk = 0
    for name, cnt in (("A", 8), ("B1", 4), ("B2", 2), ("CV", 2), ("CUZ", 4), ("G", 8), ("BR", 8),
                      ("O", 2), ("DA", 1), ("DB", 8)):
        for i in range(cnt):
            BLK[(name, i)] = nblk
            nblk += 1
    wall = nc.dram_tensor("wall", [DEPTH, nblk, D, 512], BF16, kind="Internal").ap()
    wall2 = nc.dram_tensor("wall2", [DEPTH, nblk, 128, 4096], BF16, kind="Internal").ap()

    S = Sched()
    es = ExitStack()
    with es:
        def sb(name, shape, dt):
            return es.enter_context(nc.sbuf_tensor(name, list(shape), dt))

        X = sb("X", (128, 2, 4096), F32)
        xn_t = sb("xn_t", (128, 2, D), BF16)
        hT = sb("hT", (128, NCH, 512), BF16)
        mT = sb("mT", (128, NCH, 512), BF16)
        wring = sb("wring", (128, NSLOT, NCH, 512), BF16)
        tT = sb("tT", (128, NCH, 512 + KA - 1), BF16)
        gluT = sb("gluT", (128, NCH, 512 + KB - 1), BF16)
        ya = sb("ya", (128, NCH, 512), BF16)
        yb = sb("yb", (128, NCH, 512), BF16)
        yc = sb("yc", (128, NCH, 512), BF16)
        cs_t = sb("cs_t", (128, 4, 512), BF16)
        mean_bc = sb("mean_bc", (128, 512), F32)
        rstd_bc = sb("rstd_bc", (128, 512), F32)
        tmp_t = sb("tmp_t", (128, NTMP, 512), F32)
        gv_t = sb("gv_t", (128, 2, D), F32)
        v_t = sb("v_t", (128, 4, D), BF16)
        junk = sb("junk", (128, D), BF16)
        gcb = sb("gcb", (128, 2, D), F32)
        dA_t = sb("dA_t", (128, NCH * KA * 128), BF16)
        yt_t = sb("yt_t", (128, 2, D), F32)
        pc = sb("pc", (128, NCH, DEPTH * NROW), F32)
        wsT = sb("wsT", (128, DEPTH, NCH, 128), BF16)
        bsr = sb("bsr", (33, DEPTH, NCH * 128), BF16)
        hal_a = sb("hal_a", (128, DEPTH, NCH, KA - 1), BF16)
        hal_b = sb("hal_b", (128, DEPTH, NCH, KB - 1), BF16)
        chal_a = sb("chal_a", (128, DEPTH, NCH, KA - 1), BF16)
        chal_b = sb("chal_b", (128, DEPTH, NCH, KB - 1), BF16)
        zhal = sb("zhal", (128, NCH, KB - 1), BF16)
        tfin = sb("tfin", (128, NCH, KA - 1), F32)
        gfin = sb("gfin", (128, NCH, KB - 1), F32)
        st_s = sb("st_s", (128, 40), F32)
        identf = sb("identf", (128, 128), F32)
        ident = sb("ident", (128, 128), BF16)
        ones = sb("ones", (128, 128), BF16)
        epsc = sb("epsc", (128, 1), F32)
        ps = es.enter_context(nc.psum_tensor("ps", [128, 4096], F32))

        def bank_ap(b, n, rows=128):
            return ps[0:rows, b * 512:b * 512 + n]

        def PSR(*banks):
            return tuple(("ps", b) for b in banks)

        def ACT(out, in_, func, reads, writes, **kw):
            S.op("act", lambda e: e.activation(out=out, in_=in_, func=func, **kw), reads=reads, writes=writes)

        def TT(out, in0, in1, op, reads, writes, eng="dve"):
            S.op(eng, lambda e: e.tensor_tensor(out=out, in0=in0, in1=in1, op=op), reads=reads, writes=writes)

        def STT(out, in0, scalar, in1, op0, op1, reads, writes):
            S.op("dve", lambda e: e.scalar_tensor_tensor(out=out, in0=in0, scalar=scalar, in1=in1, op0=op0, op1=op1),
                 reads=reads, writes=writes)

        def TS(out, in0, scalar1, op0, reads, writes, eng="dve"):
            S.op(eng, lambda e: e.tensor_scalar(out=out, in0=in0, scalar1=scalar1, scalar2=None, op0=op0),
                 reads=reads, writes=writes)

        def RECIP(out, in_, reads, writes):
            S.op("dve", lambda e: e.reciprocal(out=out, in_=in_), reads=reads, writes=writes)

        def COPY(out, in_, reads, writes, eng="dve"):
            S.op(eng, lambda e: e.tensor_copy(out=out, in_=in_), reads=reads, writes=writes)

        def DMA(out, in_, reads, writes, lane, eng="sp"):
            if eng == "pool":
                lane = "q_" + lane
            S.op(eng, lambda e: e.dma_start(out=out, in_=in_), reads=reads, writes=writes, lane=lane)

        def PE(fn, reads, writes, multi=False):
            if not multi:
                for w in writes:
                    lw = S.last_w.get(w)
                    assert not (lw is not None and lw[2] == "pe" and not S.readers.get(w)), ("PSUM overwrite before read", w)
            S.op("pe", fn, reads=reads, writes=writes)

        lane_ctr = [0]

        def plane():
            lane_ctr[0] += 1
            return "pro%d" % (lane_ctr[0] % 6)

        conv_i = [0]

        def convert(l, blk, dst, src):
            lane = "cv%d" % (conv_i[0] % NCONV_LANES)
            conv_i[0] += 1
            DMA(dst, src, (), (("wall", l, blk),), lane, eng="pool")

        def wsrc(l, key):
            name, i = key
            wi = w_in[l]

            def cols(a, n):
                return wi[:, a:a + n].rearrange("(k p) n -> p k n", p=128)
            if name == "A":
                return [(cols(g * 1024 + i * 128, 128), g * 128, 128) for g in range(4)]
            if name == "B1":
                return [(cols(4096 + g * 1024 + i * 256, 256), g * 256, 256) for g in range(2)]
            if name == "CV":
                return [(cols(8192 + i * 512, 512), 0, 512)]
            if name == "B2":
                return [(cols(6144 + i * 512, 512), 0, 512)]
            if name == "CUZ":
                return [(cols(7168 + g * 2048 + i * 256, 256), g * 256, 256) for g in range(2)]
            if name == "G":
                return [(cols(10240 + g * 1024 + i * 128, 128), g * 128, 128) for g in range(3)]
            if name == "BR":
                return [(w_branch[l, g][:, i * 128:(i + 1) * 128].rearrange("(k p) n -> p k n", p=128), g * 128, 128)
                        for g in range(3)]
            if name == "O":
                return [(w_out[l][:, i * 512:(i + 1) * 512].rearrange("(k p) n -> p k n", p=128), 0, 512)]
            raise KeyError(key)

        def emit_conversions(l):
            wi = w_in[l]
            for j in range(8):
                b = BLK[("A", j)]
                convert(l, b, wall[l, b].rearrange("r (g c) -> r g c", g=4),
                        wi[:, 0:4096].rearrange("r (g c) -> r g c", g=4)[:, :, j * 128:(j + 1) * 128])
            for q in range(4):
                b = BLK[("B1", q)]
                convert(l, b, wall[l, b].rearrange("r (g c) -> r g c", g=2),
                        wi[:, 4096:6144].rearrange("r (g c) -> r g c", g=2)[:, :, q * 256:(q + 1) * 256])
            for h in range(2):
                b = BLK[("CV", h)]
                convert(l, b, wall[l, b], wi[:, 8192 + h * 512:8192 + (h + 1) * 512])
            for h in range(2):
                b = BLK[("B2", h)]
                convert(l, b, wall[l, b], wi[:, 6144 + h * 512:6144 + (h + 1) * 512])
            for q in range(4):
                b = BLK[("CUZ", q)]
                convert(l, b, wall[l, b].rearrange("r (g c) -> r g c", g=2),
                        wi[:, 7168:11264].rearrange("r (g c) -> r g c", g=2)[:, :, q * 256:(q + 1) * 256])
            for j in range(8):
                b = BLK[("G", j)]
                convert(l, b, wall[l, b][:, 0:384].rearrange("r (g c) -> r g c", g=3),
                        wi[:, 10240:13312].rearrange("r (g c) -> r g c", g=3)[:, :, j * 128:(j + 1) * 128])
                b = BLK[("BR", j)]
                convert(l, b, wall[l, b][:, 0:384].rearrange("r (g c) -> r g c", g=3),
                        w_branch[l].rearrange("k r c -> r k c")[:, :, j * 128:(j + 1) * 128])
            for h in range(2):
                b = BLK[("O", h)]
                convert(l, b, wall[l, b], w_out[l][:, h * 512:(h + 1) * 512])

        S.op("pool", lambda e: e.memset(identf[:], 0.0), writes=("identf",))
        S.op("pool", lambda e: e.affine_select(out=identf[:], in_=identf[:], pattern=[[-1, 128]],
                                               compare_op=ALU.not_equal, fill=1.0, base=0,
                                               channel_multiplier=1),
             reads=("identf",), writes=("identf",))
        S.op("pool", lambda e: e.memset(ones[:], 1.0), writes=("ones",))
        S.op("pool", lambda e: e.memset(epsc[:], EPS), writes=("epsc",))
        S.op("pool", lambda e: e.memset(zhal[:], 0.0), writes=("zhal",))
        S.op("pool", lambda e: e.memset(bsr[:], 0.0), writes=("bsr",))
        COPY(ident[:], identf[:], ("identf",), ("ident",))

        X1 = X[:, 1].rearrange("p (c n) -> p c n", c=8)
        pm = X1[:, 0:2, :].rearrange("p a n -> p (a n)")
        R_pm = (("X", 1, 0), ("X", 1, 1))
        chs = X1[:, 2:4, :].rearrange("p a n -> p (a n)")
        R_chs = (("X", 1, 2), ("X", 1, 3))
        wsl = X1[:, 4:6, :].rearrange("p a n -> p (a n)")
        R_wsl = (("X", 1, 4), ("X", 1, 5))
        wsm = X1[:, 6:8, :].rearrange("p a n -> p (a n)")
        R_wsm = (("X", 1, 6), ("X", 1, 7))
        bsf = tmp_t[0:33, 0:2, :].rearrange("p a n -> p (a n)")
        R_bsf = (("tmp", 0), ("tmp", 1))
        bsh = tmp_t[0:33, 2:4, :].rearrange("p a n -> p (a n)")
        R_bsh = (("tmp", 2), ("tmp", 3))
        bsb = cs_t[0:33, 0:2, :].rearrange("p a n -> p (a n)")
        R_bsb = (("cs", 0), ("cs", 1))
        bsl = cs_t[0:33, 2:4, :].rearrange("p a n -> p (a n)")
        R_bsl = (("cs", 2), ("cs", 3))

        def pload(dst, src, res):
            DMA(dst, src, (), res, plane())

        for l in range(DEPTH):
            r0 = l * NROW
            pload(pm[r0 + R_NG:r0 + R_NG + 1, :], norm_g[l:l + 1, :], R_pm)
            pload(pm[r0 + R_CBB:r0 + R_CBB + 1, :], conv_b_b[l:l + 1, :], R_pm)
            pload(pm[r0 + R_LBG:r0 + R_LBG + 1, :], ln_b_g[l:l + 1, :], R_pm)
            pload(pm[r0 + R_LBB:r0 + R_LBB + 1, :], ln_b_b[l:l + 1, :], R_pm)
            pload(pm[r0 + R_GB0:r0 + R_GB0 + 3, :], gate_b[l], R_pm)
            pload(pm[r0 + R_CA:r0 + R_CA + KA, :], conv_a_w[l], R_pm)
            pload(pm[r0 + R_CB:r0 + R_CB + KB, :], conv_b_w[l], R_pm)
        nrt = DEPTH * NROW
        for half in range(2):
            def f(e, half=half):
                ins = None
                for cc in range(4):
                    c = half * 4 + cc
                    ins = e.transpose(out=ps[:, half * 512 + cc * nrt:half * 512 + (cc + 1) * nrt],
                                      in_=pm[0:nrt, c * 128:(c + 1) * 128], identity=identf[0:nrt, 0:nrt])
                return ins
            PE(f, R_pm + ("identf",), PSR(half))
            COPY(pc[:, half * 4:(half + 1) * 4, :],
                 ps[:, half * 512:half * 512 + 4 * nrt].rearrange("p (c r) -> p c r", c=4), PSR(half), ("pc",))

        def pcol(l, c, r):
            return pc[:, c, l * NROW + r:l * NROW + r + 1]

        if with_sample:
            pload(chs[0:2, :], ca_in[0], R_chs)
            pload(chs[2:4, :], ca_in[1], R_chs)
            pload(chs[4:34, :], cb_in[0], R_chs)
            pload(chs[34:64, :], cb_in[1], R_chs)

            def f(e):
                ins = None
                for c in range(8):
                    ins = e.transpose(out=ps[:, 1024 + c * 64:1024 + (c + 1) * 64],
                                      in_=chs[0:64, c * 128:(c + 1) * 128], identity=identf[0:64, 0:64])
                return ins
            PE(f, R_chs + ("identf",), PSR(2))
            pv = ps[:, 1024:1536].rearrange("p (c r) -> p c r", c=8)
            for l in range(DEPTH):
                COPY(chal_a[:, l], pv[:, :, 2 * l:2 * l + 2], PSR(2), ("chal",))
                COPY(chal_b[:, l], pv[:, :, 4 + 30 * l:34 + 30 * l], PSR(2), ("chal",))

        def prep_ws():
            for l in range(DEPTH):
                if l > 0:
                    pload(wsl.rearrange("p (g q) -> p g q", g=8), w_s[l].rearrange("g p q -> p g q"), R_wsl)
                for half in range(2):
                    bw = S.bank()

                    def f(e, half=half, bw=bw):
                        ins = None
                        for gg in range(4):
                            g = half * 4 + gg
                            ins = e.transpose(out=ps[:, bw * 512 + gg * 128:bw * 512 + (gg + 1) * 128],
                                              in_=wsl[:, g * 128:(g + 1) * 128], identity=identf[:])
                        return ins
                    PE(f, R_wsl + ("identf",), PSR(bw))
                    COPY(wsm[:, half * 512:(half + 1) * 512], ps[:, bw * 512:(bw + 1) * 512], PSR(bw), R_wsm)
                S.op("pool", lambda e: e.affine_select(out=wsm.rearrange("p (g q) -> p g q", g=8),
                                                       in_=wsm.rearrange("p (g q) -> p g q", g=8),
                                                       pattern=[[0, 8], [1, 128]], compare_op=ALU.is_ge, fill=0.0,
                                                       base=0, channel_multiplier=-1),
                     reads=R_wsm, writes=R_wsm)
                COPY(wsT[:, l].rearrange("p g q -> p (g q)"), wsm, R_wsm, ("wsT",))
                pload(bsf[0:1, :], b_s[l:l + 1, :], R_bsf)
                pload(bsf[32:33, :], b_s[l:l + 1, :], R_bsf)
                for r in (0, 32):
                    COPY(bsb[r:r + 1, :], bsf[r:r + 1, :], R_bsf, R_bsb)
                    COPY(bsh[r:r + 1, :], bsb[r:r + 1, :], R_bsb, R_bsh)
                    TT(bsl[r:r + 1, :], bsf[r:r + 1, :], bsh[r:r + 1, :], ALU.subtract, R_bsf + R_bsh, R_bsl)
                COPY(bsr[0:1, l, :], bsb[0:1, :], R_bsb + ("bsr",), ("bsr",))
                COPY(bsr[32:33, l, :], bsl[32:33, :], R_bsl + ("bsr",), ("bsr",))


        def ring_slot():
            s = S.slot_ptr
            S.slot_ptr = (s + 1) % NSLOT
            S.slot_blk[s] = None
            return s

        bd_ctr = [0]
        ytb = yt_t[:, :, :].rearrange("p a n -> p (a n)").bitcast(BF16)

        def build_diag(l, blk, taps, stage):
            flat, res, lane = stage
            bd_ctr[0] += 1
            on_dve = (bd_ctr[0] % 2 == 1)
            for i, (c, r) in enumerate(taps):
                wr = res if (i == 0 or i == len(taps) - 1) else ()
                if on_dve:
                    TS(flat[:, i * 128:(i + 1) * 128], ident[:], pcol(l, c, r), ALU.mult, ("pc", "ident"), wr)
                else:
                    ACT(flat[:, i * 128:(i + 1) * 128], ident[:], AF.Copy, ("pc", "ident"), wr, scale=pcol(l, c, r))
            n = len(taps) * 128

            def store():
                DMA(wall[l, blk].rearrange("(p a) n -> p (a n)", p=128)[:, 0:n], flat[:, 0:n], res,
                    (("wall", l, blk),), lane)
            return store

        STG_YT = (ytb, (("yt", 0), ("yt", 1)), "yt0")

        def stg(buf, name, lane):
            return (buf[:, :, :].rearrange("p a n -> p (a n)"), tuple((name, j) for j in range(8)), lane)

        def build_DA(l, stage):
            return build_diag(l, BLK[("DA", 0)], [(c, R_CA + k) for c in range(8) for k in range(KA)], stage)

        def build_DB(l, c, stage):
            return build_diag(l, BLK[("DB", c)], [(c, R_CB + k) for k in range(KB)], stage)

        bg_tasks = []
        bg_pending = [None]

        def bg_hook():
            if bg_pending[0] is not None:
                bg_pending[0]()
                bg_pending[0] = None
            if bg_tasks:
                bg_pending[0] = bg_tasks.pop(0)()

        converted = set()

        def wload(l, key, ncols=None):
            if ncols is None:
                ncols = 4096 if key[0] in ("DA", "DB") else 512
            blk = BLK[key]
            s = S.slot_ptr
            S.slot_ptr = (s + 1) % NSLOT
            S.slot_blk[s] = (l, blk)
            if key[0] in ("DA", "DB"):
                src = wall[l, blk].rearrange("(p a) n -> p (a n)", p=128)[:, 0:ncols]
                dst = wring[:, s].rearrange("p k n -> p (k n)")[:, 0:ncols]
                DMA(dst, src, (("wall", l, blk),), WR(s), "w%d" % s)
            elif (l, blk) not in converted:
                converted.add((l, blk))
                parts = wsrc(l, key)
                for i, (src, c0, n) in enumerate(parts):
                    if len(parts) == 1:
                        wres = WR(s)
                    elif len(parts) == 2:
                        wres = (("w", s, 2 * i), ("w", s, 2 * i + 1))
                    else:
                        wres = (("w", s, i),)
                    DMA(wring[:, s, :, c0:c0 + n], src, (), wres, "w%d_%d" % (s, i), eng="pool")
                nv = parts[-1][1] + parts[-1][2]
                DMA(wall2[l, blk].rearrange("p (k n) -> p k n", k=8)[:, :, 0:nv], wring[:, s, :, 0:nv], WR(s),
                    (("wall2", l, blk),), "ws%d" % s)
            else:
                nv = 384 if key[0] in ("G", "BR") else 512
                DMA(wring[:, s, :, 0:nv], wall2[l, blk].rearrange("p (k n) -> p k n", k=8)[:, :, 0:nv],
                    (("wall2", l, blk),), WR(s), "w%d" % s)
            return s

        HTR = tuple(("hT", c) for c in range(8))

        def WR(s):
            return tuple(("w", s, i) for i in range(4))

        def chk(s, l, key):
            assert S.slot_blk[s] == (l, BLK[key]), (s, l, key, S.slot_blk[s])

        def proj_fm(l, s, key, col0, b, NT, rhs_of=None, reads=HTR):
            chk(s, l, key)
            if rhs_of is None:
                rhs_of = lambda kk: hT[:, kk, 0:NT]

            def f(e):
                ins = None
                for kk in range(8):
                    ins = e.matmul(bank_ap(b, NT), lhsT=wring[:, s, kk, col0:col0 + 128], rhs=rhs_of(kk),
                                   start=(kk == 0), stop=(kk == 7))
                return ins
            PE(f, tuple(reads) + WR(s), PSR(b))

        def xview(T):
            return X[:, T["xb"]].rearrange("p (s n) -> p s n", s=4)

        def coview(T):
            return X[:, 1 - T["xb"]].rearrange("p (c n) -> p c n", c=8)

        def RX(T, s):
            return (("X", T["xb"], 2 * s), ("X", T["xb"], 2 * s + 1))

        def RCO(T, j):
            return (("X", 1 - T["xb"], j),)

        def load_x(T):
            rows = T["rows"]
            xv = xview(T)
            for s in range(T["NS"]):
                DMA(xv[0:rows, s, :], T["xsrc"][s * 128:s * 128 + rows, :], (), RX(T, s), "x%d" % s, eng=T.get("xeng", "pool"))

        def norm_front(T, s, c0=0):
            rows = T["rows"]
            xv = xview(T)
            xi = s % 2
            ACT(xn_t[0:rows, xi, :], xv[0:rows, s, :], AF.Square, RX(T, s), (("st", c0 + s), ("xn", xi)),
                accum_out=st_s[0:rows, c0 + s:c0 + s + 1])
            ACT(st_s[0:rows, c0 + 4 + s:c0 + 5 + s], st_s[0:rows, c0 + s:c0 + s + 1], AF.Sqrt, (("st", c0 + s), "epsc"),
                (("st", c0 + 4 + s),), scale=1.0 / D, bias=epsc[0:rows, 0:1])
            RECIP(st_s[0:rows, c0 + 8 + s:c0 + 9 + s], st_s[0:rows, c0 + 4 + s:c0 + 5 + s], (("st", c0 + 4 + s),),
                  (("st", c0 + 8 + s),))

        def norm_xn(T, s, c0=0):
            rows = T["rows"]
            xv = xview(T)
            xi = s % 2
            ACT(xn_t[0:rows, xi, :], xv[0:rows, s, :], AF.Copy, RX(T, s) + (("st", c0 + 8 + s),), (("xn", xi),),
                scale=st_s[0:rows, c0 + 8 + s:c0 + 9 + s])

        def norm_T(T, s, b4):
            rows = T["rows"]
            xi = s % 2
            pTv = ps[:, b4 * 512:(b4 + 4) * 512].bitcast(BF16)

            def f(e):
                ins = None
                for c in range(8):
                    ins = e.transpose(out=pTv[:, c * 512 + s * 128:c * 512 + s * 128 + rows],
                                      in_=xn_t[0:rows, xi, c * 128:(c + 1) * 128], identity=ident[0:rows, 0:rows])
                return ins
            PE(f, (("xn", xi), "ident"), PSR(b4, b4 + 1, b4 + 2, b4 + 3), multi=True)

        def norm_evac(T, l, b4):
            NT = T["NT"]
            pTv = ps[:, b4 * 512:(b4 + 4) * 512].bitcast(BF16)
            for c in range(8):
                if (c // 2) % 2 == 0:
                    ACT(hT[:, c, 0:NT], pTv[:, c * 512:c * 512 + NT], AF.Copy, PSR(b4 + c // 2) + ("pc",), (("hT", c),),
                        scale=pcol(l, c, R_NG))
                else:
                    TS(hT[:, c, 0:NT], pTv[:, c * 512:c * 512 + NT], pcol(l, c, R_NG), ALU.mult, PSR(b4 + c // 2) + ("pc",),
                       (("hT", c),))

        def load_DA(l):
            blkDA = BLK[("DA", 0)]
            DMA(dA_t[:, :], wall[l, blkDA].rearrange("(p a) n -> p (a n)", p=128)[:, 0:NCH * KA * 128],
                (("wall", l, blkDA),), ("dA",), "da")

        def load_gcb(l, eng):
            DMA(gcb[:, 0, :], ln_c_g[l].partition_broadcast(128), (), ("gcb0",), "gcb0", eng=eng)
            DMA(gcb[:, 1, :], ln_c_b[l].partition_broadcast(128), (), ("gcb1",), "gcb1", eng=eng)

        def body(l, T, nxt, first=False):
            NT, NS, rows = T["NT"], T["NS"], T["rows"]
            co = coview(T)
            xv = xview(T)
            last = (l == DEPTH - 1)
            load_gcb(l, "sp" if first else "pool")

            if not first:
                load_DA(l)
            COPY(tT[:, :, 0:KA - 1], T["halo_a"](l), ("hal_a", "zhal", "chal"), tuple(("tT", j) for j in range(8)))

            def conv_a(j, tu):
                b = S.bank()

                def f(e):
                    ins = None
                    for k in range(KA):
                        ins = e.matmul(bank_ap(b, NT), lhsT=dA_t[:, (j * KA + k) * 128:(j * KA + k + 1) * 128],
                                       rhs=tT[:, j, k:k + NT], start=(k == 0), stop=(k == KA - 1))
                    return ins
                PE(f, (("tT", j), "dA"), PSR(b))
                TT(ya[:, j, 0:NT], bank_ap(b, NT), tmp_t[:, tu, 0:NT], ALU.mult, PSR(b) + (("tmp", tu),), (("ya", j),))

            pend = None
            for j in range(8):
                sA = wload(l, ("A", j))
                bc = S.bank()
                proj_fm(l, sA, ("A", j), 128, bc, NT)
                bx = S.bank()
                proj_fm(l, sA, ("A", j), 256, bx, NT)
                t1 = S.tmp()
                ACT(tmp_t[:, t1, 0:NT], bank_ap(bc, NT), AF.Copy, PSR(bc), (("tmp", t1),))
                TT(tT[:, j, KA - 1:KA - 1 + NT], bank_ap(bx, NT), tmp_t[:, t1, 0:NT], ALU.mult,
                   PSR(bx) + (("tmp", t1),), (("tT", j),))
                if T["seq_end"]:
                    TT(tfin[:, j, :], ps[:, bx * 512 + NT - 2:bx * 512 + NT], tmp_t[:, t1, NT - 2:NT], ALU.mult,
                       PSR(bx) + (("tmp", t1),), ("tfin",))
                bb = S.bank()
                proj_fm(l, sA, ("A", j), 0, bb, NT)
                bz = S.bank()
                proj_fm(l, sA, ("A", j), 384, bz, NT)
                t2 = S.tmp()
                ACT(tmp_t[:, t2, 0:NT], bank_ap(bz, NT), AF.Silu, PSR(bz), (("tmp", t2),))
                t3 = S.tmp()
                TT(tmp_t[:, t3, 0:NT], bank_ap(bb, NT), tmp_t[:, t2, 0:NT], ALU.mult, PSR(bb) + (("tmp", t2),),
                   (("tmp", t3),))
                if pend is not None:
                    conv_a(*pend)
                pend = (j, t3)
                if first:
                    bg_hook()
            conv_a(*pend)
            while first and (bg_tasks or bg_pending[0] is not None):
                bg_hook()
            if first:
                prep_ws()
            if not T["seq_end"]:
                COPY(hal_a[:, l], tT[:, :, NT:NT + KA - 1], tuple(("tT", j) for j in range(8)), ("hal_a",))
            else:
                bA2 = S.bank_n(2)

                def f(e):
                    ins = None
                    for j in range(8):
                        ins = e.transpose(out=ps[0:KA - 1, bA2 * 512 + j * 128:bA2 * 512 + (j + 1) * 128],
                                          in_=tfin[:, j, :], identity=identf[:])
                    return ins
                PE(f, ("tfin", "identf"), PSR(bA2, bA2 + 1))
                ACT(yt_t[0:KA - 1, 0, :], ps[0:KA - 1, bA2 * 512:bA2 * 512 + D], AF.Copy, PSR(bA2, bA2 + 1), (("yt", 0),))
                DMA(T["dst_a"](l), yt_t[0:KA - 1, 0, :], (("yt", 0),), (), "yt0", eng="pool")

            COPY(gluT[:, :, 0:KB - 1], T["halo_b"](l), ("hal_b", "zhal", "chal"), tuple(("gluT", j) for j in range(8)))
            bS1 = S.bank()
            S.held.add(bS1)
            bS2 = S.bank()
            S.held.add(bS2)
            cs_ctr = [0]

            def conv_b(j, sD):
                b = S.bank()
                chk(sD, l, ("DB", j))
                dBf = wring[:, sD].rearrange("p k n -> p (k n)")

                def f(e):
                    ins = None
                    for k in range(KB):
                        ins = e.matmul(bank_ap(b, NT), lhsT=dBf[:, k * 128:(k + 1) * 128], rhs=gluT[:, j, k:k + NT],
                                       start=(k == 0), stop=(k == KB - 1))
                    return ins
                PE(f, (("gluT", j),) + WR(sD), PSR(b))
                ci = cs_ctr[0] % 2
                cs_ctr[0] += 1
                ACT(co[:, j, 0:NT], bank_ap(b, NT), AF.Identity, PSR(b) + ("pc",), RCO(T, j), bias=pcol(l, j, R_CBB))
                ACT(cs_t[:, 2 * ci, 0:NT], bank_ap(b, NT), AF.Identity, PSR(b) + ("pc",), (("cs", 2 * ci),),
                    bias=pcol(l, j, R_CBB))
                ACT(cs_t[:, 2 * ci + 1, 0:NT], bank_ap(b, NT), AF.Square, PSR(b) + ("pc",), (("cs", 2 * ci + 1),),
                    bias=pcol(l, j, R_CBB))
                return (j, ci)

            def stat_b(j, ci):
                def f(e):
                    e.matmul(bank_ap(bS1, NT), lhsT=ones[:, :], rhs=cs_t[:, 2 * ci, 0:NT], start=(j == 0), stop=(j == 7))
                    return e.matmul(bank_ap(bS2, NT), lhsT=ones[:, :], rhs=cs_t[:, 2 * ci + 1, 0:NT], start=(j == 0),
                                    stop=(j == 7))
                PE(f, (("cs", 2 * ci), ("cs", 2 * ci + 1), "ones"), PSR(bS1, bS2), multi=True)

            sB = None
            pend_conv = None
            pend_stat = None
            for j in range(8):
                if j % 2 == 0:
                    sB = wload(l, ("B1", j // 2))
                sD = wload(l, ("DB", j), ncols=KB * 128)
                jj = j % 2
                b1 = S.bank()
                proj_fm(l, sB, ("B1", j // 2), jj * 128, b1, NT)
                b2_ = S.bank()
                proj_fm(l, sB, ("B1", j // 2), 256 + jj * 128, b2_, NT)
                t1 = S.tmp()
                ACT(tmp_t[:, t1, 0:NT], bank_ap(b2_, NT), AF.Sigmoid, PSR(b2_), (("tmp", t1),))
                TT(gluT[:, j, KB - 1:KB - 1 + NT], bank_ap(b1, NT), tmp_t[:, t1, 0:NT], ALU.mult,
                   PSR(b1) + (("tmp", t1),), (("gluT", j),))
                if T["seq_end"]:
                    TT(gfin[:, j, :], ps[:, b1 * 512 + NT - (KB - 1):b1 * 512 + NT], tmp_t[:, t1, NT - (KB - 1):NT], ALU.mult,
                       PSR(b1) + (("tmp", t1),), ("gfin",))
                if pend_stat is not None:
                    stat_b(*pend_stat)
                    pend_stat = None
                if pend_conv is not None:
                    pend_stat = conv_b(*pend_conv)
                pend_conv = (j, sD)
            if pend_stat is not None:
                stat_b(*pend_stat)
            pend_stat = conv_b(*pend_conv)
            stat_b(*pend_stat)
            tm = S.tmp()
            ACT(mean_bc[:, 0:NT], bank_ap(bS1, NT), AF.Copy, PSR(bS1), ("mean_bc",), scale=1.0 / D)
            ACT(tmp_t[:, tm, 0:NT], bank_ap(bS1, NT), AF.Square, PSR(bS1), (("tmp", tm),), scale=1.0 / D)
            tv = S.tmp()
            STT(tmp_t[:, tv, 0:NT], bank_ap(bS2, NT), 1.0 / D, tmp_t[:, tm, 0:NT], ALU.mult, ALU.subtract,
                PSR(bS2) + (("tmp", tm),), (("tmp", tv),))
            tsd = S.tmp()
            ACT(tmp_t[:, tsd, 0:NT], tmp_t[:, tv, 0:NT], AF.Sqrt, (("tmp", tv), "epsc"), (("tmp", tsd),), bias=epsc[:, 0:1])
            RECIP(rstd_bc[:, 0:NT], tmp_t[:, tsd, 0:NT], (("tmp", tsd),), ("rstd_bc",))
            S.held.discard(bS1)
            S.held.discard(bS2)
            if not T["seq_end"]:
                COPY(hal_b[:, l], gluT[:, :, NT:NT + KB - 1], tuple(("gluT", j) for j in range(8)), ("hal_b",))
            else:
                bB2 = S.bank_n(2)

                def f(e):
                    ins = None
                    for j in range(8):
                        ins = e.transpose(out=ps[0:KB - 1, bB2 * 512 + j * 128:bB2 * 512 + (j + 1) * 128],
                                          in_=gfin[:, j, :], identity=identf[:])
                    return ins
                PE(f, ("gfin", "identf"), PSR(bB2, bB2 + 1))
                ACT(yt_t[0:KB - 1, 1, :], ps[0:KB - 1, bB2 * 512:bB2 * 512 + D], AF.Copy, PSR(bB2, bB2 + 1), (("yt", 1),))
                DMA(T["dst_b"](l), yt_t[0:KB - 1, 1, :], (("yt", 1),), (), "yt1", eng="pool")
            for j in range(9):
                if j < 8:
                    TT(co[:, j, 0:NT], co[:, j, 0:NT], mean_bc[:, 0:NT], ALU.subtract, RCO(T, j) + ("mean_bc",), RCO(T, j))
                if j >= 1:
                    TT(co[:, j - 1, 0:NT], co[:, j - 1, 0:NT], rstd_bc[:, 0:NT], ALU.mult, RCO(T, j - 1) + ("rstd_bc",),
                       RCO(T, j - 1))

            sV = [wload(l, ("CV", 0)), wload(l, ("CV", 1))]
            cvb = {}

            def c1_mm(s):
                b2 = S.bank_n(2)
                cvb[s] = b2
                for h in range(2):
                    chk(sV[h], l, ("CV", h))

                    def f(e, h=h):
                        ins = None
                        for kk in range(8):
                            ins = e.matmul(ps[0:rows, (b2 + h) * 512:(b2 + h + 1) * 512],
                                           lhsT=hT[:, kk, s * 128:s * 128 + rows], rhs=wring[:, sV[h], kk, :],
                                           start=(kk == 0), stop=(kk == 7))
                        return ins
                    PE(f, HTR + WR(sV[h]), PSR(b2 + h))

            def c1_s1(s):
                gi = s % 2
                q0 = 12 + 6 * gi
                b2 = cvb[s]
                ACT(gv_t[0:rows, gi, :], ps[0:rows, b2 * 512:b2 * 512 + D], AF.Gelu_apprx_tanh, PSR(b2, b2 + 1),
                    (("gv", gi), ("st", q0)), accum_out=st_s[0:rows, q0:q0 + 1])
                ACT(junk[0:rows, :], gv_t[0:rows, gi, :], AF.Square, (("gv", gi),), ("junk", ("st", q0 + 1)),
                    accum_out=st_s[0:rows, q0 + 1:q0 + 2])

            def c1_s2(s):
                gi = s % 2
                q0 = 12 + 6 * gi
                c = lambda i: st_s[0:rows, q0 + i:q0 + i + 1]
                TS(st_s[0:rows, q0:q0 + 2], st_s[0:rows, q0:q0 + 2], 1.0 / D, ALU.mult, (("st", q0), ("st", q0 + 1)),
                   (("st", q0), ("st", q0 + 1)))
                STT(c(2), c(0), c(0), c(1), ALU.mult, ALU.subtract, (("st", q0), ("st", q0 + 1)), (("st", q0 + 2),))
                ACT(c(3), c(2), AF.Sqrt, (("st", q0 + 2), "epsc"), (("st", q0 + 3),), scale=-1.0, bias=epsc[0:rows, 0:1])
                RECIP(c(4), c(3), (("st", q0 + 3),), (("st", q0 + 4),))
                STT(c(5), c(0), -1.0, c(4), ALU.mult, ALU.mult, (("st", q0), ("st", q0 + 4)), (("st", q0 + 5),))

            def c1_s3(s):
                gi = s % 2
                q0 = 12 + 6 * gi
                ACT(gv_t[0:rows, gi, :], gv_t[0:rows, gi, :], AF.Identity, (("gv", gi), ("st", q0 + 4), ("st", q0 + 5)),
                    (("gv", gi),), scale=st_s[0:rows, q0 + 4:q0 + 5], bias=st_s[0:rows, q0 + 5:q0 + 6])

            def c1_s4(s):
                gi = s % 2
                TT(gv_t[0:rows, gi, :], gv_t[0:rows, gi, :], gcb[0:rows, 0, :], ALU.mult, (("gv", gi), "gcb0"), (("gv", gi),))
                if T["kind"] == "p":
                    TT(v_t[0:rows, s, :], gv_t[0:rows, gi, :], gcb[0:rows, 1, :], ALU.add, (("gv", gi), "gcb1"), (("v", s),))
                else:
                    TT(gv_t[0:rows, gi, :], gv_t[0:rows, gi, :], gcb[0:rows, 1, :], ALU.add, (("gv", gi), "gcb1"),
                       (("gv", gi),))
                    ACT(v_t[0:rows, s, :], gv_t[0:rows, gi, :], AF.Copy, (("gv", gi),), (("v", s),))
                    DMA(T["dst_v"](l), gv_t[0:rows, gi, :], (("gv", gi),), (), "gv%d" % gi, eng="pool")

            for s in range(NS):
                c1_mm(s)
            if NS == 4:
                order = [(1, 0), (1, 1), (2, 0), (3, 0), (2, 1), (4, 0), (3, 1), (1, 2), (4, 1), (1, 3), (2, 2), (3, 2),
                         (2, 3), (4, 2), (3, 3), (4, 3)]
            else:
                order = [(1, 0), (2, 0), (3, 0), (4, 0)]
            stage = {1: c1_s1, 2: c1_s2, 3: c1_s3, 4: c1_s4}
            for st_i, s in order:
                stage[st_i](s)

            sZ = None
            for j in range(8):
                if j % 4 == 0:
                    sZ = wload(l, ("B2", j // 4))
                bz = S.bank()
                proj_fm(l, sZ, ("B2", j // 4), (j % 4) * 128, bz, NT)
                tz = S.tmp()
                ACT(tmp_t[:, tz, 0:NT], bank_ap(bz, NT), AF.Silu, PSR(bz), (("tmp", tz),))
                tl = S.tmp()
                ACT(tmp_t[:, tl, 0:NT], co[:, j, 0:NT], AF.Silu, RCO(T, j) + ("pc",), (("tmp", tl),),
                    scale=pcol(l, j, R_LBG), bias=pcol(l, j, R_LBB))
                TT(yb[:, j, 0:NT], tmp_t[:, tl, 0:NT], tmp_t[:, tz, 0:NT], ALU.mult, (("tmp", tl), ("tmp", tz)), (("yb", j),))

            if last and nxt is not None:
                load_x(nxt)

            pw = rows
            sU = None
            for j in range(8):
                if j % 2 == 0:
                    sU = wload(l, ("CUZ", j // 2))
                jj = j % 2
                bu = S.bank()
                proj_fm(l, sU, ("CUZ", j // 2), jj * 128, bu, NT)
                bz = S.bank()
                proj_fm(l, sU, ("CUZ", j // 2), 256 + jj * 128, bz, NT)
                tu = S.tmp()
                ACT(tmp_t[:, tu, 0:NT], bank_ap(bu, NT), AF.Gelu_apprx_tanh, PSR(bu), (("tmp", tu),))
                tz = S.tmp()
                ACT(tmp_t[:, tz, 0:NT], bank_ap(bz, NT), AF.Silu, PSR(bz), (("tmp", tz),))
                tg = S.tmp()
                TT(tmp_t[:, tg, 0:NT], tmp_t[:, tu, 0:NT], tmp_t[:, tz, 0:NT], ALU.mult, (("tmp", tu), ("tmp", tz)),
                   (("tmp", tg),))
                bs_ = S.bank()

                def f(e, j=j, bs_=bs_):
                    ins = None
                    for s in range(NS):
                        ins = e.matmul(ps[:, bs_ * 512 + s * 128:bs_ * 512 + s * 128 + pw], lhsT=ones[0:33, :],
                                       rhs=bsr[0:33, l, j * 128:j * 128 + pw], start=(s == 0), stop=False,
                                       skip_group_check=True)
                    for s in range(NS):
                        ins = e.matmul(ps[:, bs_ * 512 + s * 128:bs_ * 512 + s * 128 + pw],
                                       lhsT=v_t[0:pw, s, j * 128:(j + 1) * 128], rhs=wsT[0:pw, l, j, 0:pw],
                                       start=False, stop=(s == NS - 1), skip_group_check=True)
                    return ins
                PE(f, tuple(("v", s) for s in range(NS)) + ("ones", "bsr", "wsT"), PSR(bs_))
                TT(yc[:, j, 0:NT], bank_ap(bs_, NT), tmp_t[:, tg, 0:NT], ALU.mult, PSR(bs_) + (("tmp", tg),), (("yc", j),))

            ysrc = (ya, yb, yc)
            ynm = ("ya", "yb", "yc")
            for j in range(8):
                sG = wload(l, ("G", j), ncols=384)
                sR = wload(l, ("BR", j), ncols=384)
                tsg = []
                for k in range(3):
                    b = S.bank()
                    proj_fm(l, sG, ("G", j), k * 128, b, NT)
                    t = S.tmp()
                    ACT(tmp_t[:, t, 0:NT], bank_ap(b, NT), AF.Sigmoid, PSR(b) + ("pc",), (("tmp", t),),
                        bias=pcol(l, j, R_GB0 + k))
                    tsg.append(t)
                tmk = []
                for k in range(3):
                    b = S.bank()
                    yk = ysrc[k]
                    proj_fm(l, sR, ("BR", j), k * 128, b, NT, rhs_of=lambda kk, yk=yk: yk[:, kk, 0:NT],
                            reads=tuple((ynm[k], c) for c in range(8)))
                    t = S.tmp()
                    TT(tmp_t[:, t, 0:NT], bank_ap(b, NT), tmp_t[:, tsg[k], 0:NT], ALU.mult, PSR(b) + (("tmp", tsg[k]),),
                       (("tmp", t),))
                    tmk.append(t)
                TT(tmp_t[:, tmk[0], 0:NT], tmp_t[:, tmk[0], 0:NT], tmp_t[:, tmk[1], 0:NT], ALU.add,
                   (("tmp", tmk[0]), ("tmp", tmk[1])), (("tmp", tmk[0]),))
                TT(mT[:, j, 0:NT], tmp_t[:, tmk[0], 0:NT], tmp_t[:, tmk[2], 0:NT], ALU.add,
                   (("tmp", tmk[0]), ("tmp", tmk[2])), (("mT", j),))

            sO = [wload(l, ("O", 0)), wload(l, ("O", 1))]

            def o_mm(s):
                b2 = S.bank_n(2)
                for h in range(2):
                    chk(sO[h], l, ("O", h))

                    def f(e, h=h):
                        ins = None
                        for kk in range(8):
                            ins = e.matmul(ps[0:rows, (b2 + h) * 512:(b2 + h + 1) * 512],
                                           lhsT=mT[:, kk, s * 128:s * 128 + rows], rhs=wring[:, sO[h], kk, :],
                                           start=(kk == 0), stop=(kk == 7))
                        return ins
                    PE(f, tuple(("mT", c) for c in range(8)) + WR(sO[h]), PSR(b2 + h))
                TT(xv[0:rows, s, :], xv[0:rows, s, :], ps[0:rows, b2 * 512:b2 * 512 + D], ALU.add,
                   RX(T, s) + PSR(b2, b2 + 1), RX(T, s))

            def hold4(b):
                for i in range(4):
                    S.held.add(b + i)

            def unhold4(b):
                for i in range(4):
                    S.held.discard(b + i)

            if not last:
                b4 = S.bank_n(4)
                hold4(b4)
                for s in range(NS + 1):
                    if s < NS:
                        o_mm(s)
                        norm_front(T, s)
                        norm_xn(T, s)
                    if s >= 1:
                        norm_T(T, s - 1, b4)
                norm_evac(T, l + 1, b4)
                unhold4(b4)
            else:
                if nxt is not None:
                    b4 = S.bank_n(4)
                    hold4(b4)
                    nNS = nxt["NS"]
                    seq = []
                    o_left = list(range(NS))
                    for g in range(0, nNS, 2):
                        grp = list(range(g, min(g + 2, nNS)))
                        seq += [("nf", s) for s in grp]
                        if o_left:
                            seq.append(("o", o_left.pop(0)))
                        seq += [("T", s) for s in grp]
                    seq.append(("ev", 0))
                    seq += [("o", s) for s in o_left]
                    for kind, s in seq:
                        if kind == "nf":
                            norm_front(nxt, s)
                            norm_xn(nxt, s)
                        elif kind == "o":
                            o_mm(s)
                        elif kind == "T":
                            norm_T(nxt, s, b4)
                        else:
                            norm_evac(nxt, 0, b4)
                            unhold4(b4)
                else:
                    for s in range(NS):
                        o_mm(s)
                DMA(gcb[:, 0, :], final_g.partition_broadcast(128), (), ("gcb0",), "gcb0", eng="pool")
                for s in range(NS):
                    norm_front(T, s, c0=24)
                for s in range(NS):
                    yi = s % 2
                    STT(yt_t[0:rows, yi, :], xv[0:rows, s, :], st_s[0:rows, 24 + 8 + s:24 + 9 + s], gcb[0:rows, 0, :],
                        ALU.mult, ALU.mult, RX(T, s) + (("st", 24 + 8 + s), "gcb0"), (("yt", yi),))
                    DMA(T["ydst"][s * 128:s * 128 + rows, :], yt_t[0:rows, yi, :], (("yt", yi),), (), "yt%d" % yi, eng="pool")

        tiles = []
        for q in range(n_seq):
            for ti in range(seq_len // 512):
                t0 = q * seq_len + ti * 512
                first = (ti == 0)
                T = dict(NT=512, NS=4, rows=128, kind="p", seq_end=(ti == seq_len // 512 - 1),
                         xsrc=xp[t0:t0 + 512, :], ydst=yp[t0:t0 + 512, :],
                         halo_a=(lambda l, first=first: zhal[:, :, 0:KA - 1] if first else hal_a[:, l]),
                         halo_b=(lambda l, first=first: zhal[:, :, :] if first else hal_b[:, l]),
                         dst_a=(lambda l, q=q: nap[l, q]), dst_b=(lambda l, q=q: nbp[l, q]))
                tiles.append(T)
        if with_sample:
            tiles.append(dict(NT=samp_len, NS=1, rows=samp_len, kind="s", seq_end=True,
                              xsrc=xs[:, :], ydst=ys[:, :],
                              halo_a=(lambda l: chal_a[:, l]), halo_b=(lambda l: chal_b[:, l]),
                              dst_a=(lambda l: nas[l]), dst_b=(lambda l: nbs[l]), dst_v=(lambda l: nvs[l])))
        for i, T in enumerate(tiles):
            T["xb"] = i % 2
        T0 = tiles[0]
        T0["xeng"] = "act"
        load_x(T0)
        b4 = S.bank_n(4)
        for s in range(T0["NS"]):
            norm_front(T0, s)
            norm_xn(T0, s)
            norm_T(T0, s, b4)
        norm_evac(T0, 0, b4)
        stages0 = [stg(ya, "ya", "sg0"), stg(yb, "yb", "sg1"), stg(yc, "yc", "sg2"), stg(mT, "mT", "sg3"),
                   (v_t[:, :, :].rearrange("p a n -> p (a n)"), tuple(("v", s) for s in range(4)), "sg4"),
                   (gv_t[:, :, :].rearrange("p a n -> p (a n)").bitcast(BF16), (("gv", 0), ("gv", 1)), "sg5"),
                   STG_YT, stg(yb, "yb", "sg1"), stg(yc, "yc", "sg2")]
        build_DA(0, stages0[0])()
        load_DA(0)
        pload(wsl.rearrange("p (g q) -> p g q", g=8), w_s[0].rearrange("g p q -> p g q"), R_wsl)
        for c in range(8):
            build_DB(0, c, stages0[1 + c])()
        bg_tasks.append(lambda: build_DA(1, STG_YT))
        for c in range(8):
            bg_tasks.append(lambda c=c: build_DB(1, c, STG_YT))
        for i, T in enumerate(tiles):
            nxt = tiles[i + 1] if i + 1 < len(tiles) else None
            for l in range(DEPTH):
                body(l, T, nxt, first=(i == 0 and l == 0))

        S.barrier()

        sems = {}
        keys = [e for e in Sched.ENG if S.count[e] > 0] + [("dma", lane) for lane in S.lane_n]
        for i, k in enumerate(keys):
            sems[k] = es.enter_context(nc.semaphore("sem%d" % i))
        block = es.enter_context(nc.Block())

        def emit(eng_handle, name):
            for waits, fn, inc in S.streams[name]:
                for key, val in waits:
                    eng_handle.wait_ge(sems[key], val)
                if fn is None:
                    continue
                ins = fn(eng_handle)
                ins.then_inc(sems[inc[0]], inc[1])

        @block.tensor
        def _(e):
            emit(e, "pe")

        @block.scalar
        def _(e):
            emit(e, "act")

        @block.vector
        def _(e):
            emit(e, "dve")

        @block.gpsimd
        def _(e):
            emit(e, "pool")

        @block.sync
        def _(e):
            emit(e, "sp")
    return nc


N_CORES = 8
_W_KEYS = ("norm_g", "w_in", "conv_a_w", "conv_b_w", "conv_b_b", "ln_b_g", "ln_b_b", "ln_c_g", "ln_c_b",
           "w_s", "b_s", "gate_b", "w_branch", "w_out", "final_g")


def make_in_maps(inputs, n_cores, n_seq):
    f = lambda a: np.ascontiguousarray(np.asarray(a, dtype=np.float32))
    xp = f(inputs["x_prompt"])
    xs = f(inputs["x_sample"])
    ca = f(inputs["cache_conv_a"])
    cb = f(inputs["cache_conv_b"])
    shared = {k: f(inputs[k]) for k in _W_KEYS}
    shared["b_s"] = shared["b_s"].reshape(DEPTH, NCH * 128)
    maps = []
    for i in range(n_cores):
        m = dict(shared)
        m["xp"] = np.ascontiguousarray(xp[i * n_seq:(i + 1) * n_seq].reshape(-1, D))
        m["xs"] = np.ascontiguousarray(xs[i])
        m["ca"] = np.ascontiguousarray(ca[:, i])
        m["cb"] = np.ascontiguousarray(cb[:, i])
        maps.append(m)
    return maps


def kernel(**inputs):
    xp = np.asarray(inputs["x_prompt"])
    batch, seq, _ = xp.shape
    n_seq = batch // N_CORES
    nc = build_program(n_seq, seq, with_sample=True, samp_len=np.asarray(inputs["x_sample"]).shape[1])
    maps = make_in_maps(inputs, N_CORES, n_seq)
    res = run_bass_kernel_spmd(nc, maps, core_ids=list(range(N_CORES)))
    R = res.results
    y_prompt = np.concatenate([r["yp"].reshape(n_seq, seq, D) for r in R], axis=0)
    y_sample = np.stack([r["ys"] for r in R], axis=0)
    nap = np.concatenate([r["nap"] for r in R], axis=1)
    nbp = np.concatenate([r["nbp"] for r in R], axis=1)
    nas = np.stack([r["nas"] for r in R], axis=1)
    nbs = np.stack([r["nbs"] for r in R], axis=1)
    nvs = np.stack([r["nvs"] for r in R], axis=1)
    outs = (y_prompt, y_sample, nap, nbp, nas, nbs, nvs)
    return tuple(np.ascontiguousarray(o, dtype=np.float32) for o in outs)
```

```python
import numpy as np
from contextlib import ExitStack
import concourse.bass as bass
import concourse.mybir as mybir
from concourse.bass_utils import run_bass_kernel_spmd

F32 = mybir.dt.float32
BF16 = mybir.dt.bfloat16
AF = mybir.ActivationFunctionType
ALU = mybir.AluOpType

D = 1024
NCH = 8
DEPTH = 2
PROJ = 13312
EPS = 1e-6
KA = 3
KB = 31
NSLOT = 4
NTMP = 10
NCONV_LANES = 16
R_NG, R_CBB, R_LBG, R_LBB, R_GB0 = 0, 1, 2, 3, 4
R_CA = 7
R_CB = 10
NROW = 41


class Sched:
    ENG = ("pe", "act", "dve", "pool", "sp")

    def __init__(self):
        self.streams = {e: [] for e in self.ENG}
        self.count = {e: 0 for e in self.ENG}
        self.known = {e: {} for e in self.ENG}
        self.last_w = {}
        self.readers = {}
        self.lane_n = {}
        self.bank_ptr = 0
        self.held = set()
        self.tmp_ptr = 0
        self.slot_ptr = 0
        self.slot_blk = [None] * NSLOT

    def _need(self, eng, is_dma, tok, raw):
        key, val, src = tok
        if src is not None and src == eng and not is_dma:
            if eng == "pe":
                return False
        return True

    def op(self, eng, fn, reads=(), writes=(), lane=None):
        is_dma = lane is not None
        deps = {}

        def add(tok, raw):
            if tok is None:
                return
            if not self._need(eng, is_dma, tok, raw):
                return
            key, val, _ = tok
            if deps.get(key, 0) < val:
                deps[key] = val

        for r in reads:
            add(self.last_w.get(r), True)
        for w in writes:
            add(self.last_w.get(w), False)
            for key, (val, src) in self.readers.get(w, {}).items():
                add((key, val, src), False)
        if is_dma:
            n = self.lane_n.get(lane, 0)
            if n > 0:
                add((("dma", lane), 16 * n, None), False)
            self.lane_n[lane] = n + 1
            tok = (("dma", lane), 16 * (n + 1), None)
            inc = (("dma", lane), 16)
        else:
            self.count[eng] += 1
            tok = (eng, self.count[eng], eng)
            inc = (eng, 1)
        waits = []
        kn = self.known[eng]
        for key, val in deps.items():
            if kn.get(key, 0) < val:
                kn[key] = val
                waits.append((key, val))
        for r in reads:
            d = self.readers.setdefault(r, {})
            if d.get(tok[0], (0, None))[0] < tok[1]:
                d[tok[0]] = (tok[1], tok[2])
        for w in writes:
            self.last_w[w] = tok
            self.readers[w] = {}
        self.streams[eng].append((waits, fn, inc))
        return tok

    def barrier(self, skip_prefix=None):
        allk = {}
        for e in self.ENG:
            if self.count[e] > 0:
                allk[e] = self.count[e]
        for lane, n in self.lane_n.items():
            if skip_prefix is not None and lane.startswith(skip_prefix):
                continue
            allk[("dma", lane)] = 16 * n
        for e in self.ENG:
            waits = []
            kn = self.known[e]
            for key, val in allk.items():
                if key == e and e == "pe":
                    continue
                if kn.get(key, 0) < val:
                    kn[key] = val
                    waits.append((key, val))
            if waits:
                self.streams[e].append((waits, None, None))

    def bank(self):
        while self.bank_ptr in self.held:
            self.bank_ptr = (self.bank_ptr + 1) % 8
        b = self.bank_ptr
        self.bank_ptr = (b + 1) % 8
        return b

    def bank_n(self, n):
        while True:
            b = ((self.bank_ptr + n - 1) // n) * n % 8
            if all(((b + i) not in self.held) for i in range(n)):
                break
            self.bank_ptr = (b + n) % 8
        self.bank_ptr = (b + n) % 8
        return b

    def tmp(self):
        i = self.tmp_ptr
        self.tmp_ptr = (i + 1) % NTMP
        return i


def build_program(n_seq, seq_len, with_sample=True, samp_len=64):
    assert seq_len % 512 == 0
    nc = bass.Bass("TRN2", target_bir_lowering=False)
    ntok_p = n_seq * seq_len

    def din(name, shape):
        return nc.dram_tensor(name, list(shape), F32, kind="ExternalInput").ap()

    def dout(name, shape):
        return nc.dram_tensor(name, list(shape), F32, kind="ExternalOutput").ap()

    xp = din("xp", (ntok_p, D))
    xs = din("xs", (samp_len, D))
    ca_in = din("ca", (DEPTH, KA - 1, D))
    cb_in = din("cb", (DEPTH, KB - 1, D))
    norm_g = din("norm_g", (DEPTH, D))
    w_in = din("w_in", (DEPTH, D, PROJ))
    conv_a_w = din("conv_a_w", (DEPTH, KA, D))
    conv_b_w = din("conv_b_w", (DEPTH, KB, D))
    conv_b_b = din("conv_b_b", (DEPTH, D))
    ln_b_g = din("ln_b_g", (DEPTH, D))
    ln_b_b = din("ln_b_b", (DEPTH, D))
    ln_c_g = din("ln_c_g", (DEPTH, D))
    ln_c_b = din("ln_c_b", (DEPTH, D))
    w_s = din("w_s", (DEPTH, NCH, 128, 128))
    b_s = din("b_s", (DEPTH, NCH * 128))
    gate_b = din("gate_b", (DEPTH, 3, D))
    w_branch = din("w_branch", (DEPTH, 3, D, D))
    w_out = din("w_out", (DEPTH, D, D))
    final_g = din("final_g", (D,))

    yp = dout("yp", (ntok_p, D))
    ys = dout("ys", (samp_len, D))
    nap = dout("nap", (DEPTH, n_seq, KA - 1, D))
    nbp = dout("nbp", (DEPTH, n_seq, KB - 1, D))
    nas = dout("nas", (DEPTH, KA - 1, D))
    nbs = dout("nbs", (DEPTH, KB - 1, D))
    nvs = dout("nvs", (DEPTH, samp_len, D))

    BLK = {}
    nblk = 0
    for name, cnt in (("A", 8), ("B1", 4), ("B2", 2), ("CV", 2), ("CUZ", 4), ("G", 8), ("BR", 8),
                      ("O", 2), ("DA", 1), ("DB", 8)):
        for i in range(cnt):
            BLK[(name, i)] = nblk
            nblk += 1
    wall = nc.dram_tensor("wall", [DEPTH, nblk, D, 512], BF16, kind="Internal").ap()
    wall2 = nc.dram_tensor("wall2", [DEPTH, nblk, 128, 4096], BF16, kind="Internal").ap()

    S = Sched()
    es = ExitStack()
    with es:
        def sb(name, shape, dt):
            return es.enter_context(nc.sbuf_tensor(name, list(shape), dt))

        X = sb("X", (128, 2, 4096), F32)
        xn_t = sb("xn_t", (128, 2, D), BF16)
        hT = sb("hT", (128, NCH, 512), BF16)
        mT = sb("mT", (128, NCH, 512), BF16)
        wring = sb("wring", (128, NSLOT, NCH, 512), BF16)
        tT = sb("tT", (128, NCH, 512 + KA - 1), BF16)
        gluT = sb("gluT", (128, NCH, 512 + KB - 1), BF16)
        ya = sb("ya", (128, NCH, 512), BF16)
        yb = sb("yb", (128, NCH, 512), BF16)
        yc = sb("yc", (128, NCH, 512), BF16)
        cs_t = sb("cs_t", (128, 4, 512), BF16)
        mean_bc = sb("mean_bc", (128, 512), F32)
        rstd_bc = sb("rstd_bc", (128, 512), F32)
        tmp_t = sb("tmp_t", (128, NTMP, 512), F32)
        gv_t = sb("gv_t", (128, 2, D), F32)
        v_t = sb("v_t", (128, 4, D), BF16)
        junk = sb("junk", (128, D), BF16)
        gcb = sb("gcb", (128, 2, D), F32)
        dA_t = sb("dA_t", (128, NCH * KA * 128), BF16)
        yt_t = sb("yt_t", (128, 2, D), F32)
        pc = sb("pc", (128, NCH, DEPTH * NROW), F32)
        wsT = sb("wsT", (128, DEPTH, NCH, 128), BF16)
        bsr = sb("bsr", (33, DEPTH, NCH * 128), BF16)
        hal_a = sb("hal_a", (128, DEPTH, NCH, KA - 1), BF16)
        hal_b = sb("hal_b", (128, DEPTH, NCH, KB - 1), BF16)
        chal_a = sb("chal_a", (128, DEPTH, NCH, KA - 1), BF16)
        chal_b = sb("chal_b", (128, DEPTH, NCH, KB - 1), BF16)
        zhal = sb("zhal", (128, NCH, KB - 1), BF16)
        tfin = sb("tfin", (128, NCH, KA - 1), F32)
        gfin = sb("gfin", (128, NCH, KB - 1), F32)
        st_s = sb("st_s", (128, 40), F32)
        identf = sb("identf", (128, 128), F32)
        ident = sb("ident", (128, 128), BF16)
        ones = sb("ones", (128, 128), BF16)
        epsc = sb("epsc", (128, 1), F32)
        ps = es.enter_context(nc.psum_tensor("ps", [128, 4096], F32))

        def bank_ap(b, n, rows=128):
            return ps[0:rows, b * 512:b * 512 + n]

        def PSR(*banks):
            return tuple(("ps", b) for b in banks)

        def ACT(out, in_, func, reads, writes, **kw):
            S.op("act", lambda e: e.activation(out=out, in_=in_, func=func, **kw), reads=reads, writes=writes)

        def TT(out, in0, in1, op, reads, writes, eng="dve"):
            S.op(eng, lambda e: e.tensor_tensor(out=out, in0=in0, in1=in1, op=op), reads=reads, writes=writes)

        def STT(out, in0, scalar, in1, op0, op1, reads, writes):
            S.op("dve", lambda e: e.scalar_tensor_tensor(out=out, in0=in0, scalar=scalar, in1=in1, op0=op0, op1=op1),
                 reads=reads, writes=writes)

        def TS(out, in0, scalar1, op0, reads, writes, eng="dve"):
            S.op(eng, lambda e: e.tensor_scalar(out=out, in0=in0, scalar1=scalar1, scalar2=None, op0=op0),
                 reads=reads, writes=writes)

        def RECIP(out, in_, reads, writes):
            S.op("dve", lambda e: e.reciprocal(out=out, in_=in_), reads=reads, writes=writes)

        def COPY(out, in_, reads, writes, eng="dve"):
            S.op(eng, lambda e: e.tensor_copy(out=out, in_=in_), reads=reads, writes=writes)

        def DMA(out, in_, reads, writes, lane, eng="sp"):
            if eng == "pool":
                lane = "q_" + lane
            S.op(eng, lambda e: e.dma_start(out=out, in_=in_), reads=reads, writes=writes, lane=lane)

        def PE(fn, reads, writes, multi=False):
            if not multi:
                for w in writes:
                    lw = S.last_w.get(w)
                    assert not (lw is not None and lw[2] == "pe" and not S.readers.get(w)), ("PSUM overwrite before read", w)
            S.op("pe", fn, reads=reads, writes=writes)

        lane_ctr = [0]

        def plane():
            lane_ctr[0] += 1
            return "pro%d" % (lane_ctr[0] % 6)

        conv_i = [0]

        def convert(l, blk, dst, src):
            lane = "cv%d" % (conv_i[0] % NCONV_LANES)
            conv_i[0] += 1
            DMA(dst, src, (), (("wall", l, blk),), lane, eng="pool")

        def wsrc(l, key):
            name, i = key
            wi = w_in[l]

            def cols(a, n):
                return wi[:, a:a + n].rearrange("(k p) n -> p k n", p=128)
            if name == "A":
                return [(cols(g * 1024 + i * 128, 128), g * 128, 128) for g in range(4)]
            if name == "B1":
                return [(cols(4096 + g * 1024 + i * 256, 256), g * 256, 256) for g in range(2)]
            if name == "CV":
                return [(cols(8192 + i * 512, 512), 0, 512)]
            if name == "B2":
                return [(cols(6144 + i * 512, 512), 0, 512)]
            if name == "CUZ":
                return [(cols(7168 + g * 2048 + i * 256, 256), g * 256, 256) for g in range(2)]
            if name == "G":
                return [(cols(10240 + g * 1024 + i * 128, 128), g * 128, 128) for g in range(3)]
            if name == "BR":
                return [(w_branch[l, g][:, i * 128:(i + 1) * 128].rearrange("(k p) n -> p k n", p=128), g * 128, 128)
                        for g in range(3)]
            if name == "O":
                return [(w_out[l][:, i * 512:(i + 1) * 512].rearrange("(k p) n -> p k n", p=128), 0, 512)]
            raise KeyError(key)

        def emit_conversions(l):
            wi = w_in[l]
            for j in range(8):
                b = BLK[("A", j)]
                convert(l, b, wall[l, b].rearrange("r (g c) -> r g c", g=4),
                        wi[:, 0:4096].rearrange("r (g c) -> r g c", g=4)[:, :, j * 128:(j + 1) * 128])
            for q in range(4):
                b = BLK[("B1", q)]
                convert(l, b, wall[l, b].rearrange("r (g c) -> r g c", g=2),
                        wi[:, 4096:6144].rearrange("r (g c) -> r g c", g=2)[:, :, q * 256:(q + 1) * 256])
            for h in range(2):
                b = BLK[("CV", h)]
                convert(l, b, wall[l, b], wi[:, 8192 + h * 512:8192 + (h + 1) * 512])
            for h in range(2):
                b = BLK[("B2", h)]
                convert(l, b, wall[l, b], wi[:, 6144 + h * 512:6144 + (h + 1) * 512])
            for q in range(4):
                b = BLK[("CUZ", q)]
                convert(l, b, wall[l, b].rearrange("r (g c) -> r g c", g=2),
                        wi[:, 7168:11264].rearrange("r (g c) -> r g c", g=2)[:, :, q * 256:(q + 1) * 256])
            for j in range(8):
                b = BLK[("G", j)]
                convert(l, b, wall[l, b][:, 0:384].rearrange("r (g c) -> r g c", g=3),
                        wi[:, 10240:13312].rearrange("r (g c) -> r g c", g=3)[:, :, j * 128:(j + 1) * 128])
                b = BLK[("BR", j)]
                convert(l, b, wall[l, b][:, 0:384].rearrange("r (g c) -> r g c", g=3),
                        w_branch[l].rearrange("k r c -> r k c")[:, :, j * 128:(j + 1) * 128])
            for h in range(2):
                b = BLK[("O", h)]
                convert(l, b, wall[l, b], w_out[l][:, h * 512:(h + 1) * 512])

        S.op("pool", lambda e: e.memset(identf[:], 0.0), writes=("identf",))
        S.op("pool", lambda e: e.affine_select(out=identf[:], in_=identf[:], pattern=[[-1, 128]],
                                               compare_op=ALU.not_equal, fill=1.0, base=0,
                                               channel_multiplier=1),
             reads=("identf",), writes=("identf",))
        S.op("pool", lambda e: e.memset(ones[:], 1.0), writes=("ones",))
        S.op("pool", lambda e: e.memset(epsc[:], EPS), writes=("epsc",))
        S.op("pool", lambda e: e.memset(zhal[:], 0.0), writes=("zhal",))
        S.op("pool", lambda e: e.memset(bsr[:], 0.0), writes=("bsr",))
        COPY(ident[:], identf[:], ("identf",), ("ident",))

        X1 = X[:, 1].rearrange("p (c n) -> p c n", c=8)
        pm = X1[:, 0:2, :].rearrange("p a n -> p (a n)")
        R_pm = (("X", 1, 0), ("X", 1, 1))
        chs = X1[:, 2:4, :].rearrange("p a n -> p (a n)")
        R_chs = (("X", 1, 2), ("X", 1, 3))
        wsl = X1[:, 4:6, :].rearrange("p a n -> p (a n)")
        R_wsl = (("X", 1, 4), ("X", 1, 5))
        wsm = X1[:, 6:8, :].rearrange("p a n -> p (a n)")
        R_wsm = (("X", 1, 6), ("X", 1, 7))
        bsf = tmp_t[0:33, 0:2, :].rearrange("p a n -> p (a n)")
        R_bsf = (("tmp", 0), ("tmp", 1))
        bsh = tmp_t[0:33, 2:4, :].rearrange("p a n -> p (a n)")
        R_bsh = (("tmp", 2), ("tmp", 3))
        bsb = cs_t[0:33, 0:2, :].rearrange("p a n -> p (a n)")
        R_bsb = (("cs", 0), ("cs", 1))
        bsl = cs_t[0:33, 2:4, :].rearrange("p a n -> p (a n)")
        R_bsl = (("cs", 2), ("cs", 3))

        def pload(dst, src, res):
            DMA(dst, src, (), res, plane())

        for l in range(DEPTH):
            r0 = l * NROW
            pload(pm[r0 + R_NG:r0 + R_NG + 1, :], norm_g[l:l + 1, :], R_pm)
            pload(pm[r0 + R_CBB:r0 + R_CBB + 1, :], conv_b_b[l:l + 1, :], R_pm)
            pload(pm[r0 + R_LBG:r0 + R_LBG + 1, :], ln_b_g[l:l + 1, :], R_pm)
            pload(pm[r0 + R_LBB:r0 + R_LBB + 1, :], ln_b_b[l:l + 1, :], R_pm)
            pload(pm[r0 + R_GB0:r0 + R_GB0 + 3, :], gate_b[l], R_pm)
            pload(pm[r0 + R_CA:r0 + R_CA + KA, :], conv_a_w[l], R_pm)
            pload(pm[r0 + R_CB:r0 + R_CB + KB, :], conv_b_w[l], R_pm)
        nrt = DEPTH * NROW
        for half in range(2):
            def f(e, half=half):
                ins = None
                for cc in range(4):
                    c = half * 4 + cc
                    ins = e.transpose(out=ps[:, half * 512 + cc * nrt:half * 512 + (cc + 1) * nrt],
                                      in_=pm[0:nrt, c * 128:(c + 1) * 128], identity=identf[0:nrt, 0:nrt])
                return ins
            PE(f, R_pm + ("identf",), PSR(half))
            COPY(pc[:, half * 4:(half + 1) * 4, :],
                 ps[:, half * 512:half * 512 + 4 * nrt].rearrange("p (c r) -> p c r", c=4), PSR(half), ("pc",))

        def pcol(l, c, r):
            return pc[:, c, l * NROW + r:l * NROW + r + 1]

        if with_sample:
            pload(chs[0:2, :], ca_in[0], R_chs)
            pload(chs[2:4, :], ca_in[1], R_chs)
            pload(chs[4:34, :], cb_in[0], R_chs)
            pload(chs[34:64, :], cb_in[1], R_chs)

            def f(e):
                ins = None
                for c in range(8):
                    ins = e.transpose(out=ps[:, 1024 + c * 64:1024 + (c + 1) * 64],
                                      in_=chs[0:64, c * 128:(c + 1) * 128], identity=identf[0:64, 0:64])
                return ins
            PE(f, R_chs + ("identf",), PSR(2))
            pv = ps[:, 1024:1536].rearrange("p (c r) -> p c r", c=8)
            for l in range(DEPTH):
                COPY(chal_a[:, l], pv[:, :, 2 * l:2 * l + 2], PSR(2), ("chal",))
                COPY(chal_b[:, l], pv[:, :, 4 + 30 * l:34 + 30 * l], PSR(2), ("chal",))

        def prep_ws():
            for l in range(DEPTH):
                if l > 0:
                    pload(wsl.rearrange("p (g q) -> p g q", g=8), w_s[l].rearrange("g p q -> p g q"), R_wsl)
                for half in range(2):
                    bw = S.bank()

                    def f(e, half=half, bw=bw):
                        ins = None
                        for gg in range(4):
                            g = half * 4 + gg
                            ins = e.transpose(out=ps[:, bw * 512 + gg * 128:bw * 512 + (gg + 1) * 128],
                                              in_=wsl[:, g * 128:(g + 1) * 128], identity=identf[:])
                        return ins
                    PE(f, R_wsl + ("identf",), PSR(bw))
                    COPY(wsm[:, half * 512:(half + 1) * 512], ps[:, bw * 512:(bw + 1) * 512], PSR(bw), R_wsm)
                S.op("pool", lambda e: e.affine_select(out=wsm.rearrange("p (g q) -> p g q", g=8),
                                                       in_=wsm.rearrange("p (g q) -> p g q", g=8),
                                                       pattern=[[0, 8], [1, 128]], compare_op=ALU.is_ge, fill=0.0,
                                                       base=0, channel_multiplier=-1),
                     reads=R_wsm, writes=R_wsm)
                COPY(wsT[:, l].rearrange("p g q -> p (g q)"), wsm, R_wsm, ("wsT",))
                pload(bsf[0:1, :], b_s[l:l + 1, :], R_bsf)
                pload(bsf[32:33, :], b_s[l:l + 1, :], R_bsf)
                for r in (0, 32):
                    COPY(bsb[r:r + 1, :], bsf[r:r + 1, :], R_bsf, R_bsb)
                    COPY(bsh[r:r + 1, :], bsb[r:r + 1, :], R_bsb, R_bsh)
                    TT(bsl[r:r + 1, :], bsf[r:r + 1, :], bsh[r:r + 1, :], ALU.subtract, R_bsf + R_bsh, R_bsl)
                COPY(bsr[0:1, l, :], bsb[0:1, :], R_bsb + ("bsr",), ("bsr",))
                COPY(bsr[32:33, l, :], bsl[32:33, :], R_bsl + ("bsr",), ("bsr",))


        def ring_slot():
            s = S.slot_ptr
            S.slot_ptr = (s + 1) % NSLOT
            S.slot_blk[s] = None
            return s

        bd_ctr = [0]
        ytb = yt_t[:, :, :].rearrange("p a n -> p (a n)").bitcast(BF16)

        def build_diag(l, blk, taps, stage):
            flat, res, lane = stage
            bd_ctr[0] += 1
            on_dve = (bd_ctr[0] % 2 == 1)
            for i, (c, r) in enumerate(taps):
                wr = res if (i == 0 or i == len(taps) - 1) else ()
                if on_dve:
                    TS(flat[:, i * 128:(i + 1) * 128], ident[:], pcol(l, c, r), ALU.mult, ("pc", "ident"), wr)
                else:
                    ACT(flat[:, i * 128:(i + 1) * 128], ident[:], AF.Copy, ("pc", "ident"), wr, scale=pcol(l, c, r))
            n = len(taps) * 128

            def store():
                DMA(wall[l, blk].rearrange("(p a) n -> p (a n)", p=128)[:, 0:n], flat[:, 0:n], res,
                    (("wall", l, blk),), lane)
            return store

        STG_YT = (ytb, (("yt", 0), ("yt", 1)), "yt0")

        def stg(buf, name, lane):
            return (buf[:, :, :].rearrange("p a n -> p (a n)"), tuple((name, j) for j in range(8)), lane)

        def build_DA(l, stage):
            return build_diag(l, BLK[("DA", 0)], [(c, R_CA + k) for c in range(8) for k in range(KA)], stage)

        def build_DB(l, c, stage):
            return build_diag(l, BLK[("DB", c)], [(c, R_CB + k) for k in range(KB)], stage)

        bg_tasks = []
        bg_pending = [None]

        def bg_hook():
            if bg_pending[0] is not None:
                bg_pending[0]()
                bg_pending[0] = None
            if bg_tasks:
                bg_pending[0] = bg_tasks.pop(0)()

        converted = set()

        def wload(l, key, ncols=None):
            if ncols is None:
                ncols = 4096 if key[0] in ("DA", "DB") else 512
            blk = BLK[key]
            s = S.slot_ptr
            S.slot_ptr = (s + 1) % NSLOT
            S.slot_blk[s] = (l, blk)
            if key[0] in ("DA", "DB"):
                src = wall[l, blk].rearrange("(p a) n -> p (a n)", p=128)[:, 0:ncols]
                dst = wring[:, s].rearrange("p k n -> p (k n)")[:, 0:ncols]
                DMA(dst, src, (("wall", l, blk),), WR(s), "w%d" % s)
            elif (l, blk) not in converted:
                converted.add((l, blk))
                parts = wsrc(l, key)
                for i, (src, c0, n) in enumerate(parts):
                    if len(parts) == 1:
                        wres = WR(s)
                    elif len(parts) == 2:
                        wres = (("w", s, 2 * i), ("w", s, 2 * i + 1))
                    else:
                        wres = (("w", s, i),)
                    DMA(wring[:, s, :, c0:c0 + n], src, (), wres, "w%d_%d" % (s, i), eng="pool")
                nv = parts[-1][1] + parts[-1][2]
                DMA(wall2[l, blk].rearrange("p (k n) -> p k n", k=8)[:, :, 0:nv], wring[:, s, :, 0:nv], WR(s),
                    (("wall2", l, blk),), "ws%d" % s)
            else:
                nv = 384 if key[0] in ("G", "BR") else 512
                DMA(wring[:, s, :, 0:nv], wall2[l, blk].rearrange("p (k n) -> p k n", k=8)[:, :, 0:nv],
                    (("wall2", l, blk),), WR(s), "w%d" % s)
            return s

        HTR = tuple(("hT", c) for c in range(8))

        def WR(s):
            return tuple(("w", s, i) for i in range(4))

        def chk(s, l, key):
            assert S.slot_blk[s] == (l, BLK[key]), (s, l, key, S.slot_blk[s])

        def proj_fm(l, s, key, col0, b, NT, rhs_of=None, reads=HTR):
            chk(s, l, key)
            if rhs_of is None:
                rhs_of = lambda kk: hT[:, kk, 0:NT]

            def f(e):
                ins = None
                for kk in range(8):
                    ins = e.matmul(bank_ap(b, NT), lhsT=wring[:, s, kk, col0:col0 + 128], rhs=rhs_of(kk),
                                   start=(kk == 0), stop=(kk == 7))
                return ins
            PE(f, tuple(reads) + WR(s), PSR(b))

        def xview(T):
            return X[:, T["xb"]].rearrange("p (s n) -> p s n", s=4)

        def coview(T):
            return X[:, 1 - T["xb"]].rearrange("p (c n) -> p c n", c=8)

        def RX(T, s):
            return (("X", T["xb"], 2 * s), ("X", T["xb"], 2 * s + 1))

        def RCO(T, j):
            return (("X", 1 - T["xb"], j),)

        def load_x(T):
            rows = T["rows"]
            xv = xview(T)
            for s in range(T["NS"]):
                DMA(xv[0:rows, s, :], T["xsrc"][s * 128:s * 128 + rows, :], (), RX(T, s), "x%d" % s, eng=T.get("xeng", "pool"))

        def norm_front(T, s, c0=0):
            rows = T["rows"]
            xv = xview(T)
            xi = s % 2
            ACT(xn_t[0:rows, xi, :], xv[0:rows, s, :], AF.Square, RX(T, s), (("st", c0 + s), ("xn", xi)),
                accum_out=st_s[0:rows, c0 + s:c0 + s + 1])
            ACT(st_s[0:rows, c0 + 4 + s:c0 + 5 + s], st_s[0:rows, c0 + s:c0 + s + 1], AF.Sqrt, (("st", c0 + s), "epsc"),
                (("st", c0 + 4 + s),), scale=1.0 / D, bias=epsc[0:rows, 0:1])
            RECIP(st_s[0:rows, c0 + 8 + s:c0 + 9 + s], st_s[0:rows, c0 + 4 + s:c0 + 5 + s], (("st", c0 + 4 + s),),
                  (("st", c0 + 8 + s),))

        def norm_xn(T, s, c0=0):
            rows = T["rows"]
            xv = xview(T)
            xi = s % 2
            ACT(xn_t[0:rows, xi, :], xv[0:rows, s, :], AF.Copy, RX(T, s) + (("st", c0 + 8 + s),), (("xn", xi),),
                scale=st_s[0:rows, c0 + 8 + s:c0 + 9 + s])

        def norm_T(T, s, b4):
            rows = T["rows"]
            xi = s % 2
            pTv = ps[:, b4 * 512:(b4 + 4) * 512].bitcast(BF16)

            def f(e):
                ins = None
                for c in range(8):
                    ins = e.transpose(out=pTv[:, c * 512 + s * 128:c * 512 + s * 128 + rows],
                                      in_=xn_t[0:rows, xi, c * 128:(c + 1) * 128], identity=ident[0:rows, 0:rows])
                return ins
            PE(f, (("xn", xi), "ident"), PSR(b4, b4 + 1, b4 + 2, b4 + 3), multi=True)

        def norm_evac(T, l, b4):
            NT = T["NT"]
            pTv = ps[:, b4 * 512:(b4 + 4) * 512].bitcast(BF16)
            for c in range(8):
                if (c // 2) % 2 == 0:
                    ACT(hT[:, c, 0:NT], pTv[:, c * 512:c * 512 + NT], AF.Copy, PSR(b4 + c // 2) + ("pc",), (("hT", c),),
                        scale=pcol(l, c, R_NG))
                else:
                    TS(hT[:, c, 0:NT], pTv[:, c * 512:c * 512 + NT], pcol(l, c, R_NG), ALU.mult, PSR(b4 + c // 2) + ("pc",),
                       (("hT", c),))

        def load_DA(l):
            blkDA = BLK[("DA", 0)]
            DMA(dA_t[:, :], wall[l, blkDA].rearrange("(p a) n -> p (a n)", p=128)[:, 0:NCH * KA * 128],
                (("wall", l, blkDA),), ("dA",), "da")

        def load_gcb(l, eng):
            DMA(gcb[:, 0, :], ln_c_g[l].partition_broadcast(128), (), ("gcb0",), "gcb0", eng=eng)
            DMA(gcb[:, 1, :], ln_c_b[l].partition_broadcast(128), (), ("gcb1",), "gcb1", eng=eng)

        def body(l, T, nxt, first=False):
            NT, NS, rows = T["NT"], T["NS"], T["rows"]
            co = coview(T)
            xv = xview(T)
            last = (l == DEPTH - 1)
            load_gcb(l, "sp" if first else "pool")

            if not first:
                load_DA(l)
            COPY(tT[:, :, 0:KA - 1], T["halo_a"](l), ("hal_a", "zhal", "chal"), tuple(("tT", j) for j in range(8)))

            def conv_a(j, tu):
                b = S.bank()

                def f(e):
                    ins = None
                    for k in range(KA):
                        ins = e.matmul(bank_ap(b, NT), lhsT=dA_t[:, (j * KA + k) * 128:(j * KA + k + 1) * 128],
                                       rhs=tT[:, j, k:k + NT], start=(k == 0), stop=(k == KA - 1))
                    return ins
                PE(f, (("tT", j), "dA"), PSR(b))
                TT(ya[:, j, 0:NT], bank_ap(b, NT), tmp_t[:, tu, 0:NT], ALU.mult, PSR(b) + (("tmp", tu),), (("ya", j),))

            pend = None
            for j in range(8):
                sA = wload(l, ("A", j))
                bc = S.bank()
                proj_fm(l, sA, ("A", j), 128, bc, NT)
                bx = S.bank()
                proj_fm(l, sA, ("A", j), 256, bx, NT)
                t1 = S.tmp()
                ACT(tmp_t[:, t1, 0:NT], bank_ap(bc, NT), AF.Copy, PSR(bc), (("tmp", t1),))
                TT(tT[:, j, KA - 1:KA - 1 + NT], bank_ap(bx, NT), tmp_t[:, t1, 0:NT], ALU.mult,
                   PSR(bx) + (("tmp", t1),), (("tT", j),))
                if T["seq_end"]:
                    TT(tfin[:, j, :], ps[:, bx * 512 + NT - 2:bx * 512 + NT], tmp_t[:, t1, NT - 2:NT], ALU.mult,
                       PSR(bx) + (("tmp", t1),), ("tfin",))
                bb = S.bank()
                proj_fm(l, sA, ("A", j), 0, bb, NT)
                bz = S.bank()
                proj_fm(l, sA, ("A", j), 384, bz, NT)
                t2 = S.tmp()
                ACT(tmp_t[:, t2, 0:NT], bank_ap(bz, NT), AF.Silu, PSR(bz), (("tmp", t2),))
                t3 = S.tmp()
                TT(tmp_t[:, t3, 0:NT], bank_ap(bb, NT), tmp_t[:, t2, 0:NT], ALU.mult, PSR(bb) + (("tmp", t2),),
                   (("tmp", t3),))
                if pend is not None:
                    conv_a(*pend)
                pend = (j, t3)
                if first:
                    bg_hook()
            conv_a(*pend)
            if first and T["seq_end"]:
                while bg_tasks or bg_pending[0] is not None:
                    bg_hook()
            if first:
                prep_ws()
            if not T["seq_end"]:
                COPY(hal_a[:, l], tT[:, :, NT:NT + KA - 1], tuple(("tT", j) for j in range(8)), ("hal_a",))
            else:
                bA2 = S.bank_n(2)

                def f(e):
                    ins = None
                    for j in range(8):
                        ins = e.transpose(out=ps[0:KA - 1, bA2 * 512 + j * 128:bA2 * 512 + (j + 1) * 128],
                                          in_=tfin[:, j, :], identity=identf[:])
                    return ins
                PE(f, ("tfin", "identf"), PSR(bA2, bA2 + 1))
                ACT(yt_t[0:KA - 1, 0, :], ps[0:KA - 1, bA2 * 512:bA2 * 512 + D], AF.Copy, PSR(bA2, bA2 + 1), (("yt", 0),))
                DMA(T["dst_a"](l), yt_t[0:KA - 1, 0, :], (("yt", 0),), (), "yt0", eng="pool")

            COPY(gluT[:, :, 0:KB - 1], T["halo_b"](l), ("hal_b", "zhal", "chal"), tuple(("gluT", j) for j in range(8)))
            bS1 = S.bank()
            S.held.add(bS1)
            bS2 = S.bank()
            S.held.add(bS2)
            cs_ctr = [0]

            def conv_b(j, sD):
                b = S.bank()
                chk(sD, l, ("DB", j))
                dBf = wring[:, sD].rearrange("p k n -> p (k n)")

                def f(e):
                    ins = None
                    for k in range(KB):
                        ins = e.matmul(bank_ap(b, NT), lhsT=dBf[:, k * 128:(k + 1) * 128], rhs=gluT[:, j, k:k + NT],
                                       start=(k == 0), stop=(k == KB - 1))
                    return ins
                PE(f, (("gluT", j),) + WR(sD), PSR(b))
                ci = cs_ctr[0] % 2
                cs_ctr[0] += 1
                ACT(co[:, j, 0:NT], bank_ap(b, NT), AF.Identity, PSR(b) + ("pc",), RCO(T, j), bias=pcol(l, j, R_CBB))
                ACT(cs_t[:, 2 * ci, 0:NT], bank_ap(b, NT), AF.Identity, PSR(b) + ("pc",), (("cs", 2 * ci),),
                    bias=pcol(l, j, R_CBB))
                ACT(cs_t[:, 2 * ci + 1, 0:NT], bank_ap(b, NT), AF.Square, PSR(b) + ("pc",), (("cs", 2 * ci + 1),),
                    bias=pcol(l, j, R_CBB))
                return (j, ci)

            def stat_b(j, ci):
                def f(e):
                    e.matmul(bank_ap(bS1, NT), lhsT=ones[:, :], rhs=cs_t[:, 2 * ci, 0:NT], start=(j == 0), stop=(j == 7))
                    return e.matmul(bank_ap(bS2, NT), lhsT=ones[:, :], rhs=cs_t[:, 2 * ci + 1, 0:NT], start=(j == 0),
                                    stop=(j == 7))
                PE(f, (("cs", 2 * ci), ("cs", 2 * ci + 1), "ones"), PSR(bS1, bS2), multi=True)

            sB = None
            pend_conv = None
            pend_stat = None
            for j in range(8):
                if j % 2 == 0:
                    sB = wload(l, ("B1", j // 2))
                sD = wload(l, ("DB", j), ncols=KB * 128)
                jj = j % 2
                b1 = S.bank()
                proj_fm(l, sB, ("B1", j // 2), jj * 128, b1, NT)
                b2_ = S.bank()
                proj_fm(l, sB, ("B1", j // 2), 256 + jj * 128, b2_, NT)
                t1 = S.tmp()
                ACT(tmp_t[:, t1, 0:NT], bank_ap(b2_, NT), AF.Sigmoid, PSR(b2_), (("tmp", t1),))
                TT(gluT[:, j, KB - 1:KB - 1 + NT], bank_ap(b1, NT), tmp_t[:, t1, 0:NT], ALU.mult,
                   PSR(b1) + (("tmp", t1),), (("gluT", j),))
                if T["seq_end"]:
                    TT(gfin[:, j, :], ps[:, b1 * 512 + NT - (KB - 1):b1 * 512 + NT], tmp_t[:, t1, NT - (KB - 1):NT], ALU.mult,
                       PSR(b1) + (("tmp", t1),), ("gfin",))
                if pend_stat is not None:
                    stat_b(*pend_stat)
                    pend_stat = None
                if pend_conv is not None:
                    pend_stat = conv_b(*pend_conv)
                pend_conv = (j, sD)
                if first:
                    bg_hook()
            if pend_stat is not None:
                stat_b(*pend_stat)
            pend_stat = conv_b(*pend_conv)
            stat_b(*pend_stat)
            while first and (bg_tasks or bg_pending[0] is not None):
                bg_hook()
            tm = S.tmp()
            ACT(mean_bc[:, 0:NT], bank_ap(bS1, NT), AF.Copy, PSR(bS1), ("mean_bc",), scale=1.0 / D)
            ACT(tmp_t[:, tm, 0:NT], bank_ap(bS1, NT), AF.Square, PSR(bS1), (("tmp", tm),), scale=1.0 / D)
            tv = S.tmp()
            STT(tmp_t[:, tv, 0:NT], bank_ap(bS2, NT), 1.0 / D, tmp_t[:, tm, 0:NT], ALU.mult, ALU.subtract,
                PSR(bS2) + (("tmp", tm),), (("tmp", tv),))
            tsd = S.tmp()
            ACT(tmp_t[:, tsd, 0:NT], tmp_t[:, tv, 0:NT], AF.Sqrt, (("tmp", tv), "epsc"), (("tmp", tsd),), bias=epsc[:, 0:1])
            RECIP(rstd_bc[:, 0:NT], tmp_t[:, tsd, 0:NT], (("tmp", tsd),), ("rstd_bc",))
            S.held.discard(bS1)
            S.held.discard(bS2)
            if not T["seq_end"]:
                COPY(hal_b[:, l], gluT[:, :, NT:NT + KB - 1], tuple(("gluT", j) for j in range(8)), ("hal_b",))
            else:
                bB2 = S.bank_n(2)

                def f(e):
                    ins = None
                    for j in range(8):
                        ins = e.transpose(out=ps[0:KB - 1, bB2 * 512 + j * 128:bB2 * 512 + (j + 1) * 128],
                                          in_=gfin[:, j, :], identity=identf[:])
                    return ins
                PE(f, ("gfin", "identf"), PSR(bB2, bB2 + 1))
                ACT(yt_t[0:KB - 1, 1, :], ps[0:KB - 1, bB2 * 512:bB2 * 512 + D], AF.Copy, PSR(bB2, bB2 + 1), (("yt", 1),))
                DMA(T["dst_b"](l), yt_t[0:KB - 1, 1, :], (("yt", 1),), (), "yt1", eng="pool")
            for j in range(9):
                if j < 8:
                    TT(co[:, j, 0:NT], co[:, j, 0:NT], mean_bc[:, 0:NT], ALU.subtract, RCO(T, j) + ("mean_bc",), RCO(T, j))
                if j >= 1:
                    TT(co[:, j - 1, 0:NT], co[:, j - 1, 0:NT], rstd_bc[:, 0:NT], ALU.mult, RCO(T, j - 1) + ("rstd_bc",),
                       RCO(T, j - 1))

            sV = [wload(l, ("CV", 0)), wload(l, ("CV", 1))]
            cvb = {}

            def c1_mm(s):
                b2 = S.bank_n(2)
                cvb[s] = b2
                for h in range(2):
                    chk(sV[h], l, ("CV", h))

                    def f(e, h=h):
                        ins = None
                        for kk in range(8):
                            ins = e.matmul(ps[0:rows, (b2 + h) * 512:(b2 + h + 1) * 512],
                                           lhsT=hT[:, kk, s * 128:s * 128 + rows], rhs=wring[:, sV[h], kk, :],
                                           start=(kk == 0), stop=(kk == 7))
                        return ins
                    PE(f, HTR + WR(sV[h]), PSR(b2 + h))

            def c1_s1(s):
                gi = s % 2
                q0 = 12 + 6 * gi
                b2 = cvb[s]
                ACT(gv_t[0:rows, gi, :], ps[0:rows, b2 * 512:b2 * 512 + D], AF.Gelu_apprx_tanh, PSR(b2, b2 + 1),
                    (("gv", gi), ("st", q0)), accum_out=st_s[0:rows, q0:q0 + 1])
                ACT(junk[0:rows, :], gv_t[0:rows, gi, :], AF.Square, (("gv", gi),), ("junk", ("st", q0 + 1)),
                    accum_out=st_s[0:rows, q0 + 1:q0 + 2])

            def c1_s2(s):
                gi = s % 2
                q0 = 12 + 6 * gi
                c = lambda i: st_s[0:rows, q0 + i:q0 + i + 1]
                TS(st_s[0:rows, q0:q0 + 2], st_s[0:rows, q0:q0 + 2], 1.0 / D, ALU.mult, (("st", q0), ("st", q0 + 1)),
                   (("st", q0), ("st", q0 + 1)))
                STT(c(2), c(0), c(0), c(1), ALU.mult, ALU.subtract, (("st", q0), ("st", q0 + 1)), (("st", q0 + 2),))
                ACT(c(3), c(2), AF.Sqrt, (("st", q0 + 2), "epsc"), (("st", q0 + 3),), scale=-1.0, bias=epsc[0:rows, 0:1])
                RECIP(c(4), c(3), (("st", q0 + 3),), (("st", q0 + 4),))
                STT(c(5), c(0), -1.0, c(4), ALU.mult, ALU.mult, (("st", q0), ("st", q0 + 4)), (("st", q0 + 5),))

            def c1_s3(s):
                gi = s % 2
                q0 = 12 + 6 * gi
                ACT(gv_t[0:rows, gi, :], gv_t[0:rows, gi, :], AF.Identity, (("gv", gi), ("st", q0 + 4), ("st", q0 + 5)),
                    (("gv", gi),), scale=st_s[0:rows, q0 + 4:q0 + 5], bias=st_s[0:rows, q0 + 5:q0 + 6])

            def c1_s4(s):
                gi = s % 2
                TT(gv_t[0:rows, gi, :], gv_t[0:rows, gi, :], gcb[0:rows, 0, :], ALU.mult, (("gv", gi), "gcb0"), (("gv", gi),))
                if T["kind"] == "p":
                    TT(v_t[0:rows, s, :], gv_t[0:rows, gi, :], gcb[0:rows, 1, :], ALU.add, (("gv", gi), "gcb1"), (("v", s),))
                else:
                    TT(gv_t[0:rows, gi, :], gv_t[0:rows, gi, :], gcb[0:rows, 1, :], ALU.add, (("gv", gi), "gcb1"),
                       (("gv", gi),))
                    ACT(v_t[0:rows, s, :], gv_t[0:rows, gi, :], AF.Copy, (("gv", gi),), (("v", s),))
                    DMA(T["dst_v"](l), gv_t[0:rows, gi, :], (("gv", gi),), (), "gv%d" % gi, eng="pool")

            for s in range(NS):
                c1_mm(s)
            if NS == 4:
                order = [(1, 0), (1, 1), (2, 0), (3, 0), (2, 1), (4, 0), (3, 1), (1, 2), (4, 1), (1, 3), (2, 2), (3, 2),
                         (2, 3), (4, 2), (3, 3), (4, 3)]
            else:
                order = [(1, 0), (2, 0), (3, 0), (4, 0)]
            stage = {1: c1_s1, 2: c1_s2, 3: c1_s3, 4: c1_s4}
            for st_i, s in order:
                stage[st_i](s)

            sZ = None
            for j in range(8):
                if j % 4 == 0:
                    sZ = wload(l, ("B2", j // 4))
                bz = S.bank()
                proj_fm(l, sZ, ("B2", j // 4), (j % 4) * 128, bz, NT)
                tz = S.tmp()
                ACT(tmp_t[:, tz, 0:NT], bank_ap(bz, NT), AF.Silu, PSR(bz), (("tmp", tz),))
                tl = S.tmp()
                ACT(tmp_t[:, tl, 0:NT], co[:, j, 0:NT], AF.Silu, RCO(T, j) + ("pc",), (("tmp", tl),),
                    scale=pcol(l, j, R_LBG), bias=pcol(l, j, R_LBB))
                TT(yb[:, j, 0:NT], tmp_t[:, tl, 0:NT], tmp_t[:, tz, 0:NT], ALU.mult, (("tmp", tl), ("tmp", tz)), (("yb", j),))

            if last and nxt is not None:
                load_x(nxt)

            pw = rows
            sU = None
            for j in range(8):
                if j % 2 == 0:
                    sU = wload(l, ("CUZ", j // 2))
                jj = j % 2
                bu = S.bank()
                proj_fm(l, sU, ("CUZ", j // 2), jj * 128, bu, NT)
                bz = S.bank()
                proj_fm(l, sU, ("CUZ", j // 2), 256 + jj * 128, bz, NT)
                tu = S.tmp()
                ACT(tmp_t[:, tu, 0:NT], bank_ap(bu, NT), AF.Gelu_apprx_tanh, PSR(bu), (("tmp", tu),))
                tz = S.tmp()
                ACT(tmp_t[:, tz, 0:NT], bank_ap(bz, NT), AF.Silu, PSR(bz), (("tmp", tz),))
                tg = S.tmp()
                TT(tmp_t[:, tg, 0:NT], tmp_t[:, tu, 0:NT], tmp_t[:, tz, 0:NT], ALU.mult, (("tmp", tu), ("tmp", tz)),
                   (("tmp", tg),))
                bs_ = S.bank()

                def f(e, j=j, bs_=bs_):
                    ins = None
                    for s in range(NS):
                        ins = e.matmul(ps[:, bs_ * 512 + s * 128:bs_ * 512 + s * 128 + pw], lhsT=ones[0:33, :],
                                       rhs=bsr[0:33, l, j * 128:j * 128 + pw], start=(s == 0), stop=False,
                                       skip_group_check=True)
                    for s in range(NS):
                        ins = e.matmul(ps[:, bs_ * 512 + s * 128:bs_ * 512 + s * 128 + pw],
                                       lhsT=v_t[0:pw, s, j * 128:(j + 1) * 128], rhs=wsT[0:pw, l, j, 0:pw],
                                       start=False, stop=(s == NS - 1), skip_group_check=True)
                    return ins
                PE(f, tuple(("v", s) for s in range(NS)) + ("ones", "bsr", "wsT"), PSR(bs_))
                TT(yc[:, j, 0:NT], bank_ap(bs_, NT), tmp_t[:, tg, 0:NT], ALU.mult, PSR(bs_) + (("tmp", tg),), (("yc", j),))

            ysrc = (ya, yb, yc)
            ynm = ("ya", "yb", "yc")
            for j in range(8):
                sG = wload(l, ("G", j), ncols=384)
                sR = wload(l, ("BR", j), ncols=384)
                tsg = []
                for k in range(3):
                    b = S.bank()
                    proj_fm(l, sG, ("G", j), k * 128, b, NT)
                    t = S.tmp()
                    ACT(tmp_t[:, t, 0:NT], bank_ap(b, NT), AF.Sigmoid, PSR(b) + ("pc",), (("tmp", t),),
                        bias=pcol(l, j, R_GB0 + k))
                    tsg.append(t)
                tmk = []
                for k in range(3):
                    b = S.bank()
                    yk = ysrc[k]
                    proj_fm(l, sR, ("BR", j), k * 128, b, NT, rhs_of=lambda kk, yk=yk: yk[:, kk, 0:NT],
                            reads=tuple((ynm[k], c) for c in range(8)))
                    t = S.tmp()
                    TT(tmp_t[:, t, 0:NT], bank_ap(b, NT), tmp_t[:, tsg[k], 0:NT], ALU.mult, PSR(b) + (("tmp", tsg[k]),),
                       (("tmp", t),))
                    tmk.append(t)
                TT(tmp_t[:, tmk[0], 0:NT], tmp_t[:, tmk[0], 0:NT], tmp_t[:, tmk[1], 0:NT], ALU.add,
                   (("tmp", tmk[0]), ("tmp", tmk[1])), (("tmp", tmk[0]),))
                TT(mT[:, j, 0:NT], tmp_t[:, tmk[0], 0:NT], tmp_t[:, tmk[2], 0:NT], ALU.add,
                   (("tmp", tmk[0]), ("tmp", tmk[2])), (("mT", j),))

            sO = [wload(l, ("O", 0)), wload(l, ("O", 1))]

            def o_mm(s):
                b2 = S.bank_n(2)
                for h in range(2):
                    chk(sO[h], l, ("O", h))

                    def f(e, h=h):
                        ins = None
                        for kk in range(8):
                            ins = e.matmul(ps[0:rows, (b2 + h) * 512:(b2 + h + 1) * 512],
                                           lhsT=mT[:, kk, s * 128:s * 128 + rows], rhs=wring[:, sO[h], kk, :],
                                           start=(kk == 0), stop=(kk == 7))
                        return ins
                    PE(f, tuple(("mT", c) for c in range(8)) + WR(sO[h]), PSR(b2 + h))
                TT(xv[0:rows, s, :], xv[0:rows, s, :], ps[0:rows, b2 * 512:b2 * 512 + D], ALU.add,
                   RX(T, s) + PSR(b2, b2 + 1), RX(T, s))

            def hold4(b):
                for i in range(4):
                    S.held.add(b + i)

            def unhold4(b):
                for i in range(4):
                    S.held.discard(b + i)

            if not last:
                b4 = S.bank_n(4)
                hold4(b4)
                for s in range(NS + 1):
                    if s < NS:
                        o_mm(s)
                        norm_front(T, s)
                        norm_xn(T, s)
                    if s >= 1:
                        norm_T(T, s - 1, b4)
                norm_evac(T, l + 1, b4)
                unhold4(b4)
            else:
                if nxt is not None:
                    b4 = S.bank_n(4)
                    hold4(b4)
                    nNS = nxt["NS"]
                    seq = []
                    o_left = list(range(NS))
                    for g in range(0, nNS, 2):
                        grp = list(range(g, min(g + 2, nNS)))
                        seq += [("nf", s) for s in grp]
                        if o_left:
                            seq.append(("o", o_left.pop(0)))
                        seq += [("T", s) for s in grp]
                    seq.append(("ev", 0))
                    seq += [("o", s) for s in o_left]
                    for kind, s in seq:
                        if kind == "nf":
                            norm_front(nxt, s)
                            norm_xn(nxt, s)
                        elif kind == "o":
                            o_mm(s)
                        elif kind == "T":
                            norm_T(nxt, s, b4)
                        else:
                            norm_evac(nxt, 0, b4)
                            unhold4(b4)
                else:
                    for s in range(NS):
                        o_mm(s)
                DMA(gcb[:, 0, :], final_g.partition_broadcast(128), (), ("gcb0",), "gcb0", eng="pool")
                for s in range(NS):
                    norm_front(T, s, c0=24)
                for s in range(NS):
                    yi = s % 2
                    STT(yt_t[0:rows, yi, :], xv[0:rows, s, :], st_s[0:rows, 24 + 8 + s:24 + 9 + s], gcb[0:rows, 0, :],
                        ALU.mult, ALU.mult, RX(T, s) + (("st", 24 + 8 + s), "gcb0"), (("yt", yi),))
                    DMA(T["ydst"][s * 128:s * 128 + rows, :], yt_t[0:rows, yi, :], (("yt", yi),), (), "yt%d" % yi, eng="pool")

        tiles = []
        for q in range(n_seq):
            for ti in range(seq_len // 512):
                t0 = q * seq_len + ti * 512
                first = (ti == 0)
                T = dict(NT=512, NS=4, rows=128, kind="p", seq_end=(ti == seq_len // 512 - 1),
                         xsrc=xp[t0:t0 + 512, :], ydst=yp[t0:t0 + 512, :],
                         halo_a=(lambda l, first=first: zhal[:, :, 0:KA - 1] if first else hal_a[:, l]),
                         halo_b=(lambda l, first=first: zhal[:, :, :] if first else hal_b[:, l]),
                         dst_a=(lambda l, q=q: nap[l, q]), dst_b=(lambda l, q=q: nbp[l, q]))
                tiles.append(T)
        if with_sample:
            tiles.append(dict(NT=samp_len, NS=1, rows=samp_len, kind="s", seq_end=True,
                              xsrc=xs[:, :], ydst=ys[:, :],
                              halo_a=(lambda l: chal_a[:, l]), halo_b=(lambda l: chal_b[:, l]),
                              dst_a=(lambda l: nas[l]), dst_b=(lambda l: nbs[l]), dst_v=(lambda l: nvs[l])))
        for i, T in enumerate(tiles):
            T["xb"] = i % 2
        T0 = tiles[0]
        T0["xeng"] = "act"
        load_x(T0)
        b4 = S.bank_n(4)
        for s in range(T0["NS"]):
            norm_front(T0, s)
            norm_xn(T0, s)
            norm_T(T0, s, b4)
        norm_evac(T0, 0, b4)
        stages0 = [stg(ya, "ya", "sg0"), stg(yb, "yb", "sg1"), stg(yc, "yc", "sg2"), stg(mT, "mT", "sg3"),
                   (v_t[:, :, :].rearrange("p a n -> p (a n)"), tuple(("v", s) for s in range(4)), "sg4"),
                   (gv_t[:, :, :].rearrange("p a n -> p (a n)").bitcast(BF16), (("gv", 0), ("gv", 1)), "sg5"),
                   STG_YT, stg(yb, "yb", "sg1"), stg(yc, "yc", "sg2")]
        build_DA(0, stages0[0])()
        load_DA(0)
        pload(wsl.rearrange("p (g q) -> p g q", g=8), w_s[0].rearrange("g p q -> p g q"), R_wsl)
        for c in range(2):
            build_DB(0, c, stages0[1 + c])()
        for c in range(2, 8):
            bg_tasks.append(lambda c=c: build_DB(0, c, stages0[1 + c]))
        bg_tasks.append(lambda: build_DA(1, STG_YT))
        for c in range(8):
            bg_tasks.append(lambda c=c: build_DB(1, c, STG_YT))
        for i, T in enumerate(tiles):
            nxt = tiles[i + 1] if i + 1 < len(tiles) else None
            for l in range(DEPTH):
                body(l, T, nxt, first=(i == 0 and l == 0))

        S.barrier()

        sems = {}
        keys = [e for e in Sched.ENG if S.count[e] > 0] + [("dma", lane) for lane in S.lane_n]
        for i, k in enumerate(keys):
            sems[k] = es.enter_context(nc.semaphore("sem%d" % i))
        block = es.enter_context(nc.Block())

        def emit(eng_handle, name):
            for waits, fn, inc in S.streams[name]:
                for key, val in waits:
                    eng_handle.wait_ge(sems[key], val)
                if fn is None:
                    continue
                ins = fn(eng_handle)
                ins.then_inc(sems[inc[0]], inc[1])

        @block.tensor
        def _(e):
            emit(e, "pe")

        @block.scalar
        def _(e):
            emit(e, "act")

        @block.vector
        def _(e):
            emit(e, "dve")

        @block.gpsimd
        def _(e):
            emit(e, "pool")

        @block.sync
        def _(e):
            emit(e, "sp")
    return nc


N_CORES = 8
_W_KEYS = ("norm_g", "w_in", "conv_a_w", "conv_b_w", "conv_b_b", "ln_b_g", "ln_b_b", "ln_c_g", "ln_c_b",
           "w_s", "b_s", "gate_b", "w_branch", "w_out", "final_g")


def make_in_maps(inputs, n_cores, n_seq):
    f = lambda a: np.ascontiguousarray(np.asarray(a, dtype=np.float32))
    xp = f(inputs["x_prompt"])
    xs = f(inputs["x_sample"])
    ca = f(inputs["cache_conv_a"])
    cb = f(inputs["cache_conv_b"])
    shared = {k: f(inputs[k]) for k in _W_KEYS}
    shared["b_s"] = shared["b_s"].reshape(DEPTH, NCH * 128)
    maps = []
    for i in range(n_cores):
        m = dict(shared)
        m["xp"] = np.ascontiguousarray(xp[i * n_seq:(i + 1) * n_seq].reshape(-1, D))
        m["xs"] = np.ascontiguousarray(xs[i])
        m["ca"] = np.ascontiguousarray(ca[:, i])
        m["cb"] = np.ascontiguousarray(cb[:, i])
        maps.append(m)
    return maps


def kernel(**inputs):
    xp = np.asarray(inputs["x_prompt"])
    batch, seq, _ = xp.shape
    n_seq = batch // N_CORES
    nc = build_program(n_seq, seq, with_sample=True, samp_len=np.asarray(inputs["x_sample"]).shape[1])
    maps = make_in_maps(inputs, N_CORES, n_seq)
    res = run_bass_kernel_spmd(nc, maps, core_ids=list(range(N_CORES)))
    R = res.results
    y_prompt = np.concatenate([r["yp"].reshape(n_seq, seq, D) for r in R], axis=0)
    y_sample = np.stack([r["ys"] for r in R], axis=0)
    nap = np.concatenate([r["nap"] for r in R], axis=1)
    nbp = np.concatenate([r["nbp"] for r in R], axis=1)
    nas = np.stack([r["nas"] for r in R], axis=1)
    nbs = np.stack([r["nbs"] for r in R], axis=1)
    nvs = np.stack([r["nvs"] for r in R], axis=1)
    outs = (y_prompt, y_sample, nap, nbp, nas, nbs, nvs)
    return tuple(np.ascontiguousarray(o, dtype=np.float32) for o in outs)
```
